# Optimizing a Trainium2 kernel written in Bass

```python
import jax, jax.numpy as jnp
from jax import lax
import numpy as np

D_MODEL = 1024
BATCH = 8
SEQ = 2048
DEPTH = 1

GRID_W = 64
CTX_LEN = 256
LRU_WIDTH = D_MODEL
LRU_HEADS = 8
LRU_HEAD_DIM = LRU_WIDTH // LRU_HEADS
CONV_WIDTH = 4
CONV_PAD_LEFT = 1
LRU_C = 8.0
SGU_WIDTH = D_MODEL
SGU_GROUPS = 8
SGU_GROUP_DIM = SGU_WIDTH // SGU_GROUPS
CHUNK = 128
D_MIX = LRU_WIDTH + SGU_WIDTH
D_IN = 2 * LRU_WIDTH + 3 * SGU_WIDTH
NORM_EPS = 1e-6
LN_EPS = 1e-5

kernel_name = "hybrid_rglru_chunk_sgu_dit_block"


def rmsnorm(x, g):
    xf = x.astype(jnp.float32)
    y = xf * lax.rsqrt(jnp.mean(xf * xf, axis=-1, keepdims=True) + NORM_EPS)
    return (y * g.astype(jnp.float32)).astype(x.dtype)


def ada_mod(cond, w, b):
    m = jax.nn.silu(cond) @ w + b
    return jnp.split(m, 3, axis=-1)


def project(h, shift, scale, norm_g, w_in):
    hn = rmsnorm(h, norm_g) * (1.0 + scale) + shift
    return hn @ w_in


def short_conv(xa, w, b):
    L = xa.shape[1]
    xp = jnp.pad(xa, ((0, 0), (CONV_PAD_LEFT, CONV_WIDTH - 1 - CONV_PAD_LEFT), (0, 0)))
    y = xp[:, 0:L] * w[0]
    for k in range(1, CONV_WIDTH):
        y = y + xp[:, k:k + L] * w[k]
    return y + b


def _lin_combine(e1, e2):
    a1, b1 = e1
    a2, b2 = e2
    return a1 * a2, a2 * b1 + b2


def rglru_direction(xc, h0, wa, ba, wx, bx, lam, reverse):
    Bn, L, _ = xc.shape
    xh = xc.reshape(Bn, L, LRU_HEADS, LRU_HEAD_DIM)
    r = jax.nn.sigmoid(jnp.einsum('blhi,hij->blhj', xh, wa) + ba).reshape(Bn, L, LRU_WIDTH)
    i = jax.nn.sigmoid(jnp.einsum('blhi,hij->blhj', xh, wx) + bx).reshape(Bn, L, LRU_WIDTH)
    log_a = -LRU_C * r * jax.nn.softplus(-lam.astype(jnp.float32))
    a = jnp.exp(log_a)
    u = jnp.sqrt(-jnp.expm1(2.0 * log_a)) * (i * xc)
    a_cum, h = lax.associative_scan(_lin_combine, (a, u), reverse=reverse, axis=1)
    h = h + a_cum * h0[:, None, :]
    final = h[:, 0] if reverse else h[:, -1]
    return h, final


def rglru_bidir(xc, h0f, h0b, wa, ba, wx, bx, lam):
    hf, ff = rglru_direction(xc, h0f, wa[0], ba[0], wx[0], bx[0], lam[0], False)
    hb, fb = rglru_direction(xc, h0b, wa[1], ba[1], wx[1], bx[1], lam[1], True)
    return hf + hb, ff, fb


def chunk_sgu(u, v, ln_g, ln_b, w_s, b_s, n_chunks):
    Bn = u.shape[0]
    vf = v.astype(jnp.float32)
    mu = jnp.mean(vf, axis=-1, keepdims=True)
    var = jnp.mean(jnp.square(vf - mu), axis=-1, keepdims=True)
    vn = (vf - mu) * lax.rsqrt(var + LN_EPS) * ln_g + ln_b
    vc = vn.reshape(Bn, n_chunks, CHUNK, SGU_GROUPS, SGU_GROUP_DIM)
    mixed = jnp.einsum('gpq,bnqgc->bnpgc', w_s, vc) + b_s.T[None, None, :, :, None]
    return u * mixed.reshape(Bn, n_chunks * CHUNK, SGU_WIDTH).astype(u.dtype)


def split_proj(z):
    W, S = LRU_WIDTH, SGU_WIDTH
    return (z[..., :W], z[..., W:2 * W], z[..., 2 * W:2 * W + S],
            z[..., 2 * W + S:2 * W + 2 * S], z[..., 2 * W + 2 * S:])


def mixer_out(y_lru, ga, y_sgu, gb, w_out):
    y = jnp.concatenate([y_lru * jax.nn.silu(ga), y_sgu * jax.nn.silu(gb)], axis=-1)
    return y @ w_out


def setup_inputs(seed: int = 0) -> dict:
    key = jax.random.key(seed)
    ks = jax.random.split(key, 24)
    nrm = jax.random.normal
    a_c = jax.random.uniform(ks[13], (DEPTH, 2, LRU_WIDTH), minval=0.9, maxval=0.999)
    s = a_c ** (1.0 / LRU_C)
    return {
        "x": nrm(ks[0], (BATCH, SEQ, D_MODEL)),
        "c": nrm(ks[1], (BATCH, D_MODEL)),
        "ctx": nrm(ks[2], (BATCH, CTX_LEN, D_MODEL)),
        "c_ctx": nrm(ks[3], (D_MODEL,)),
        "ada_w": nrm(ks[4], (DEPTH, D_MODEL, 3 * D_MODEL)) * (0.5 * D_MODEL ** -0.5),
        "ada_b": 0.01 * nrm(ks[5], (DEPTH, 3 * D_MODEL)),
        "norm_g": 1.0 + 0.05 * nrm(ks[6], (DEPTH, D_MODEL)),
        "w_in": nrm(ks[7], (DEPTH, D_MODEL, D_IN)) * D_MODEL ** -0.5,
        "conv_w": nrm(ks[8], (DEPTH, CONV_WIDTH, LRU_WIDTH)) * CONV_WIDTH ** -0.5,
        "conv_b": 0.01 * nrm(ks[9], (DEPTH, LRU_WIDTH)),
        "lru_wa": nrm(ks[10], (DEPTH, 2, LRU_HEADS, LRU_HEAD_DIM, LRU_HEAD_DIM)) * LRU_HEAD_DIM ** -0.5,
        "lru_ba": 0.01 * nrm(ks[11], (DEPTH, 2, LRU_HEADS, LRU_HEAD_DIM)),
        "lru_wx": nrm(ks[12], (DEPTH, 2, LRU_HEADS, LRU_HEAD_DIM, LRU_HEAD_DIM)) * LRU_HEAD_DIM ** -0.5,
        "lru_bx": 0.01 * nrm(ks[14], (DEPTH, 2, LRU_HEADS, LRU_HEAD_DIM)),
        "lru_lambda": jnp.log(s) - jnp.log1p(-s),
        "sgu_ln_g": 1.0 + 0.05 * nrm(ks[15], (DEPTH, SGU_WIDTH)),
        "sgu_ln_b": 0.01 * nrm(ks[16], (DEPTH, SGU_WIDTH)),
        "sgu_w": nrm(ks[17], (DEPTH, SGU_GROUPS, CHUNK, CHUNK)) * (0.5 * CHUNK ** -0.5),
        "sgu_b": 1.0 + 0.1 * nrm(ks[18], (DEPTH, SGU_GROUPS, CHUNK)),
        "w_out": nrm(ks[19], (DEPTH, D_MIX, D_MODEL)) * D_MIX ** -0.5,
        "final_g": 1.0 + 0.05 * nrm(ks[20], (D_MODEL,)),
    }


def reference(x, c, ctx, c_ctx, ada_w, ada_b, norm_g, w_in, conv_w, conv_b, lru_wa, lru_ba,
              lru_wx, lru_bx, lru_lambda, sgu_ln_g, sgu_ln_b, sgu_w, sgu_b, w_out, final_g):
    Bn, L, _ = x.shape
    rows = L // GRID_W
    n_chunks = rows * GRID_W // CHUNK
    n_ctx_chunks = ctx.shape[1] // CHUNK
    zeros = jnp.zeros((Bn, LRU_WIDTH), jnp.float32)
    for layer in range(DEPTH):
        sh_x, sc_x, g_x = ada_mod(c[:, None, :], ada_w[layer], ada_b[layer])
        sh_c, sc_c, g_c = ada_mod(c_ctx[None, None, :], ada_w[layer], ada_b[layer])
        lru_p = (lru_wa[layer], lru_ba[layer], lru_wx[layer], lru_bx[layer], lru_lambda[layer])
        last = layer == DEPTH - 1

        ctx_cols = LRU_WIDTH if last else D_IN
        zc = project(ctx, sh_c, sc_c, norm_g[layer], w_in[layer][:, :ctx_cols])
        xc_c = short_conv(zc[..., :LRU_WIDTH], conv_w[layer], conv_b[layer]).astype(jnp.float32)
        y_c, hf_c, hb_c = rglru_bidir(xc_c, zeros, zeros, *lru_p)

        zx = project(x, sh_x, sc_x, norm_g[layer], w_in[layer])
        xa_x, ga_x, u_x, v_x, gb_x = split_proj(zx)
        xc_x = short_conv(xa_x, conv_w[layer], conv_b[layer]).astype(jnp.float32)
        y_l, _, _ = rglru_bidir(xc_x, hf_c, hb_c, *lru_p)
        y_s = chunk_sgu(jax.nn.gelu(u_x), jax.nn.gelu(v_x), sgu_ln_g[layer], sgu_ln_b[layer],
                        sgu_w[layer], sgu_b[layer], n_chunks)
        x_new = x + g_x * mixer_out(y_l.astype(x.dtype), ga_x, y_s, gb_x, w_out[layer])

        if not last:
            _, ga_c, u_c, v_c, gb_c = split_proj(zc)
            y_sc = chunk_sgu(jax.nn.gelu(u_c), jax.nn.gelu(v_c), sgu_ln_g[layer], sgu_ln_b[layer],
                             sgu_w[layer], sgu_b[layer], n_ctx_chunks)
            ctx = ctx + g_c * mixer_out(y_c.astype(ctx.dtype), ga_c, y_sc, gb_c, w_out[layer])
        x = x_new
    return rmsnorm(x, final_g)
```

```python
import contextlib
import numpy as np
import concourse.bass as bass
import concourse.mybir as mybir
from concourse.bass_utils import run_bass_kernel_spmd
from concourse.alu_op_type import AluOpType as ALU

F32 = mybir.dt.float32
BF16 = mybir.dt.bfloat16
AF = mybir.ActivationFunctionType

D = 1024
T = 2048
C = 256
TT = T + C
NCORES = 8
ENGS = ['sync', 'scalar', 'vector', 'gpsimd', 'tensor']


class Sched:
    def __init__(self):
        self.all = []
        self.cnt = {}
        self.lastw = {}
        self.readers = {}

    def op(self, eng, fn, r=(), w=(), dma=None):
        key = ('d', dma) if dma else ('c', eng)
        inc = 16 if dma else 1
        self.cnt[key] = self.cnt.get(key, 0) + inc
        tok = (key, self.cnt[key])
        deps = {}

        def add(k, v):
            if deps.get(k, 0) < v:
                deps[k] = v
        xs_ = [s for s in r if isinstance(s, tuple) and s[0] == 'ps']
        r = [s for s in r if not (isinstance(s, tuple) and s[0] == 'ps')]
        for s in r:
            t = self.lastw.get(s)
            if t: add(t[0], t[1])
        for s in w:
            t = self.lastw.get(s)
            if t: add(t[0], t[1])
            for k, v in self.readers.get(s, {}).items():
                add(k, v)
        for s in xs_:
            t = self.lastw.get(s)
            if t and not (t[0] == key and t[2]):
                add(t[0], t[1])
            for k, v in self.readers.get(s, {}).items():
                add(k, v)
        for s in r:
            rd = self.readers.setdefault(s, {})
            if rd.get(key, 0) < tok[1]:
                rd[key] = tok[1]
        for s in w:
            self.lastw[s] = (tok[0], tok[1], False)
            self.readers[s] = {}
        for s in xs_:
            self.lastw[s] = (tok[0], tok[1], True)
            self.readers[s] = {}
        self.all.append((eng, fn, deps, key, inc))
        return tok

    def barrier(self):
        allc = dict(self.cnt)
        for e in ENGS:
            self.all.append((e, None, dict(allc), None, 0))
        self.lastw = {}
        self.readers = {}

    def emit(self, nc, block, sems, limit=None):
        ops = self.all if limit is None else self.all[:limit]
        final = {}
        for (eng, fn, deps, key, inc) in ops:
            if fn is not None:
                final[key] = final.get(key, 0) + inc
        for eng in ENGS:
            q = [o for o in ops if o[0] == eng]

            def body(e, q=q, eng=eng):
                waited = {}
                for (_, fn, deps, key, inc) in q:
                    for k, v in deps.items():
                        if eng == 'tensor' and k == ('c', 'tensor'):
                            continue
                        if waited.get(k, 0) < v:
                            e.wait_ge(sems[k], v)
                            waited[k] = v
                    if fn is not None:
                        ins = fn(e)
                        ins.then_inc(sems[key], inc)
                for k, v in final.items():
                    if waited.get(k, 0) < v:
                        e.wait_ge(sems[k], v)
            getattr(block, eng)(body)


def build_nc():
    nc = bass.Bass("TRN2", target_bir_lowering=False)

    def din(name, shape):
        return nc.dram_tensor(name, list(shape), F32, kind="ExternalInput").ap()
    x_d = din("x", [T, D])
    ctx_d = din("ctx", [C, D])
    cT_d = din("cT", [128, 16])
    adaw_d = din("ada_w", [D, 3 * D])
    adabT_d = din("adabT", [128, 24])
    bgrep_d = din("bg_rep", [128, D])
    fgrep_d = din("fg_rep", [128, D])
    ngT_d = din("ngT", [128, 8])
    win_d = din("w_in", [D, 5 * D])
    wout_d = din("w_out", [2 * D, D])
    cwT_d = din("cwT", [128, 32])
    cbT_d = din("cbT", [128, 8])
    wg_d = din("wg", [128, 4096])
    baT_d = din("baT", [128, 16])
    bxT_d = din("bxT", [128, 16])
    lamT_d = din("lamT", [128, 16])
    lngT_d = din("lngT", [128, 8])
    lnbT_d = din("lnbT", [128, 8])
    wsT_d = din("wsT", [128, 1024])
    sbrep_d = din("sb_rep", [128, 1024])
    out_d = nc.dram_tensor("out", [T, D], F32, kind="ExternalOutput").ap()

    S = Sched()
    late = {}
    winv = win_d.rearrange("(k p) n -> p k n", p=128)
    woutv = wout_d.rearrange("(k p) n -> p k n", p=128)
    adawv = adaw_d.rearrange("(k p) n -> p k n", p=128)

    P_CT, P_ADAB, P_NG, P_CW, P_CB, P_BA, P_BX, P_LAM, P_LNG, P_LNB = 0, 16, 40, 48, 80, 88, 104, 120, 136, 144
    NPAR = 152
    Q_CS, Q_MODX, Q_MODC, Q_GMX, Q_GMC, Q_E1, Q_SP, Q_CL, Q_CH, Q_HBA, Q_HBX, Q_NEGH = 0, 16, 32, 48, 56, 64, 80, 96, 112, 128, 144, 160
    NDER = 168
    NB = TT

    def k3(t, n):
        return t.rearrange("p (k n) -> p k n", n=n)

    with contextlib.ExitStack() as _es:
        par = _es.enter_context(nc.sbuf_tensor("par", [128, NPAR], F32))
        der = _es.enter_context(nc.sbuf_tensor("der", [128, NDER], F32))
        stat = _es.enter_context(nc.sbuf_tensor("stat", [128, 4 * 24], F32))
        ident = _es.enter_context(nc.sbuf_tensor("ident", [128, 128], F32))
        ones_b = _es.enter_context(nc.sbuf_tensor("ones_b", [128, 128], BF16))
        csbf = _es.enter_context(nc.sbuf_tensor("csbf", [128, 16], BF16))
        gxb = _es.enter_context(nc.sbuf_tensor("gxb", [128, D], F32))
        xmT = _es.enter_context(nc.sbuf_tensor("xmT", [128, 8 * TT], BF16))
        Ylru = _es.enter_context(nc.sbuf_tensor("Ylru", [128, 8 * T], BF16))
        ps = _es.enter_context(nc.psum_tensor("ps", [128, 4096], F32))

        def bank(i, c0=0, c1=512):
            return ps[:, i * 512 + c0:i * 512 + c1]

        def ld(dst, src, w, r=()):
            S.op('sync', lambda e: e.dma_start(out=dst, in_=src), r=r, w=w, dma=('par', w[0]))

        with contextlib.ExitStack() as _es:
            Wg = _es.enter_context(nc.sbuf_tensor("Wg", [128, 4096], BF16))
            Wxs = _es.enter_context(nc.sbuf_tensor("Wxs", [128, 2 * 1024], BF16))
            xa_c = _es.enter_context(nc.sbuf_tensor("xa_c", [128, C + 3], F32))
            xa_x = _es.enter_context(nc.sbuf_tensor("xa_x", [128, T + 3], F32))
            xcA = _es.enter_context(nc.sbuf_tensor("xcA", [128, NB], F32))
            xcbA = _es.enter_context(nc.sbuf_tensor("xcbA", [128, NB], BF16))
            ibc = [0]
            def wxa_load(h):
                S.op('gpsimd', lambda e, h=h: e.dma_start(out=k3(Wxs[:, (h % 2) * 1024:(h % 2 + 1) * 1024], 128), in_=winv[:, :, h * 128:(h + 1) * 128]),
                     w=[('Wxs', h % 2)], dma=('wc', 'Wxs', h % 2))

            xcs = [xcA, None]
            xcbs = [xcbA, None]

            def stPa(h):
                hp = h % 2
                for j in range(5):
                    c0, c1 = (0, C) if j == 0 else (C + (j - 1) * 512, C + j * 512)
                    n = c1 - c0
                    bk = 6 + (ibc[0] % 2)
                    ibc[0] += 1

                    def xa_mm(e, c0=c0, c1=c1, n=n, bk=bk):
                        i = None
                        for k in range(8):
                            i = e.matmul(bank(bk, 0, n), lhsT=Wxs[:, hp * 1024 + k * 128:hp * 1024 + (k + 1) * 128],
                                         rhs=xmT[:, k * TT + c0:k * TT + c1], start=(k == 0), stop=(k == 7))
                        return i
                    S.op('tensor', xa_mm, r=[('Wxs', hp)] + [('xmT', g2, k) for g2 in range(c0 // 256, (c1 + 255) // 256) for k in range(8)], w=[('ps', bk)])
                    if j == 0:
                        S.op('scalar', lambda e, bk=bk: e.activation(out=xa_c[:, 1:1 + C], in_=bank(bk, 0, C), func=AF.Identity), r=[('ps', bk)], w=['xa_c'])
                    else:
                        S.op('scalar', lambda e, bk=bk, j=j: e.activation(out=xa_x[:, 1 + (j - 1) * 512:1 + j * 512], in_=bank(bk), func=AF.Identity), r=[('ps', bk)], w=['xa_x'])
                if h + 2 < 8:
                    wxa_load(h + 2)

            def stPc(h, part, extra_w=()):
                hp = h % 2
                xo = hp * NB
                (src, sl, L, o0) = ((xa_c, 'xa_c', C, 0), (xa_x, 'xa_x', T, C))[part]
                S.op('vector', lambda e: e.tensor_scalar(
                    out=xcs[hp][:, o0:o0 + L], in0=src[:, 0:L], scalar1=par[:, P_CW + h * 4:P_CW + h * 4 + 1], scalar2=par[:, P_CB + h:P_CB + h + 1],
                    op0=ALU.mult, op1=ALU.add), r=[sl, 'p_cw', 'p_cb'], w=[('xc', hp, o0)] + list(extra_w))
                for tp in range(1, 4):
                    S.op('vector', lambda e, tp=tp: e.scalar_tensor_tensor(
                        out=xcs[hp][:, o0:o0 + L], in0=src[:, tp:tp + L], scalar=par[:, P_CW + h * 4 + tp:P_CW + h * 4 + tp + 1], in1=xcs[hp][:, o0:o0 + L],
                        op0=ALU.mult, op1=ALU.add), r=[sl, 'p_cw', ('xc', hp, o0)], w=[('xc', hp, o0)])
                if part == 1:
                    S.op('vector', lambda e: e.tensor_copy(out=xcbs[hp][:], in_=xcs[hp][:]), r=[('xc', hp, 0), ('xc', hp, C)], w=[('xcb', hp)] + list(extra_w))


            with contextlib.ExitStack() as _es:
                adaW = _es.enter_context(nc.sbuf_tensor("adaW", [128, 8 * 3072], BF16))
                Wgv = _es.enter_context(nc.sbuf_tensor("Wga", [128, 8 * 1024], BF16))
                xin = _es.enter_context(nc.sbuf_tensor("xin", [128, 4 * D], F32))
                xs = _es.enter_context(nc.sbuf_tensor("xs", [128, 2 * D], F32))
                junk = _es.enter_context(nc.sbuf_tensor("junk", [128, D], BF16))
                csb = _es.enter_context(nc.sbuf_tensor("csb", [128, 8 * 128], BF16))
                bgrep = _es.enter_context(nc.sbuf_tensor("bgrep", [128, 1024], F32))
                ld(par[:, P_CT:P_CT + 16], cT_d[:, :], ['p_ct'])
                ld(par[:, P_ADAB:P_ADAB + 24], adabT_d[:, :], ['p_adab'])
                ld(par[:, P_NG:P_NG + 8], ngT_d[:, :], ['p_ng'])
                for sl in range(2):
                    S.op('gpsimd', lambda e, sl=sl: e.dma_start(out=k3(adaW[:], 3072)[:, :, sl * 1024:(sl + 1) * 1024], in_=adawv[:, :, sl * 1024:(sl + 1) * 1024]),
                         w=[('adaW', sl)], dma=('wc', 'adaW', sl))
                S.op('gpsimd', lambda e: e.dma_start(out=k3(Wgv[:], 1024), in_=winv[:, :, 1024:2048]), w=['Wgv'], dma=('wc', 'Wgv'))

                def late_loads():
                    S.op('gpsimd', lambda e: e.dma_start(out=k3(adaW[:], 3072)[:, :, 2048:3072], in_=adawv[:, :, 2048:3072]), w=[('adaW', 2)], dma=('wc', 'adaW', 2))
                    for hh in range(2):
                        S.op('gpsimd', lambda e, hh=hh: e.dma_start(out=Wg[:, hh * 2048:(hh + 1) * 2048], in_=wg_d[:, hh * 2048:(hh + 1) * 2048]),
                             w=[('Wg', hh)], dma=('wc', 'Wg', hh))
                    wxa_load(0)
                    wxa_load(1)
                ld(par[:, P_LAM:P_LAM + 16], lamT_d[:, :], ['p_lam'])
                ld(par[:, P_BA:P_BA + 16], baT_d[:, :], ['p_ba'])
                ld(par[:, P_BX:P_BX + 16], bxT_d[:, :], ['p_bx'])
                ld(par[:, P_CW:P_CW + 32], cwT_d[:, :], ['p_cw'])
                ld(par[:, P_CB:P_CB + 8], cbT_d[:, :], ['p_cb'])
                ld(par[:, P_LNG:P_LNG + 8], lngT_d[:, :], ['p_lng'])
                ld(par[:, P_LNB:P_LNB + 8], lnbT_d[:, :], ['p_lnb'])
                ld(bgrep[:], bgrep_d[:, :], ['bgrep'])

                S.op('gpsimd', lambda e: e.memset(ident[:], 0.0), w=['ident'])
                S.op('gpsimd', lambda e: e.affine_select(out=ident[:], in_=ident[:], pattern=[[-1, 128]], compare_op=ALU.not_equal,
                                                         fill=1.0, base=0, channel_multiplier=1), r=['ident'], w=['ident'])
                S.op('gpsimd', lambda e: e.memset(ones_b[:], 1.0), w=['ones_b'])
                S.op('gpsimd', lambda e: e.memset(xa_c[:], 0.0), w=['xa_c'])
                S.op('gpsimd', lambda e: e.memset(xa_x[:], 0.0), w=['xa_x'])
                S.op('gpsimd', lambda e: e.memset(der[:, Q_NEGH:Q_NEGH + 1], -0.5), w=['negh'])

                S.op('scalar', lambda e: e.activation(out=csbf[:], in_=par[:, P_CT:P_CT + 16], func=AF.Silu), r=['p_ct'], w=['cs'])
                S.op('scalar', lambda e: e.activation(out=der[:, Q_E1:Q_E1 + 16], in_=par[:, P_LAM:P_LAM + 16], func=AF.Exp, scale=-1.0), r=['p_lam'], w=['e1'])
                S.op('scalar', lambda e: e.activation(out=der[:, Q_SP:Q_SP + 16], in_=der[:, Q_E1:Q_E1 + 16], func=AF.Ln, bias=1.0, scale=1.0), r=['e1'], w=['sp'])
                S.op('vector', lambda e: e.tensor_scalar(out=der[:, Q_CL:Q_CL + 16], in0=der[:, Q_SP:Q_SP + 16], scalar1=-8.0, scalar2=None, op0=ALU.mult), r=['sp'], w=['cl'])
                S.op('vector', lambda e: e.tensor_scalar(out=der[:, Q_CH:Q_CH + 16], in0=der[:, Q_SP:Q_SP + 16], scalar1=-4.0, scalar2=None, op0=ALU.mult), r=['sp'], w=['ch'])
                S.op('vector', lambda e: e.tensor_scalar(out=der[:, Q_HBA:Q_HBA + 16], in0=par[:, P_BA:P_BA + 16], scalar1=0.5, scalar2=None, op0=ALU.mult), r=['p_ba'], w=['hba'])
                S.op('vector', lambda e: e.tensor_scalar(out=der[:, Q_HBX:Q_HBX + 16], in0=par[:, P_BX:P_BX + 16], scalar1=0.5, scalar2=None, op0=ALU.mult), r=['p_bx'], w=['hbx'])
                for k in range(8):
                    S.op('vector', lambda e, k=k: e.tensor_scalar(out=csb[:, k * 128:(k + 1) * 128], in0=ones_b[:], scalar1=csbf[:, 2 * k:2 * k + 1],
                                                                  scalar2=None, op0=ALU.mult), r=['ones_b', 'cs'], w=[('csb', k)])

                def ada_stage():
                    def ada_mm(e):
                        i = None
                        for fc in range(16):
                            for k in range(8):
                                i = e.matmul(ps[:, 7 * 512 + fc * 2:7 * 512 + fc * 2 + 2], lhsT=adaW[:, k * 3072 + fc * 128:k * 3072 + (fc + 1) * 128],
                                             rhs=csbf[:, 2 * k:2 * k + 2], start=(k == 0), stop=(k == 7), skip_group_check=True)
                        return i
                    S.op('tensor', ada_mm, r=[('adaW', 0), ('adaW', 1), 'cs'], w=[('ps', 7)])
                    modps = ps[:, 7 * 512:7 * 512 + 32].rearrange("p (f v) -> p f v", v=2)
                    S.op('vector', lambda e: e.tensor_tensor(out=der[:, Q_MODX:Q_MODX + 16], in0=modps[:, :, 0], in1=par[:, P_ADAB:P_ADAB + 16], op=ALU.add),
                         r=[('ps', 7), 'p_adab'], w=['modx'])
                    S.op('vector', lambda e: e.tensor_tensor(out=der[:, Q_MODC:Q_MODC + 16], in0=modps[:, :, 1], in1=par[:, P_ADAB:P_ADAB + 16], op=ALU.add),
                         r=[('ps', 7), 'p_adab'], w=['modc'])
                    S.op('vector', lambda e: e.scalar_tensor_tensor(out=der[:, Q_GMX:Q_GMX + 8], in0=der[:, Q_MODX + 8:Q_MODX + 16], scalar=1.0, in1=par[:, P_NG:P_NG + 8],
                                                                    op0=ALU.add, op1=ALU.mult), r=['modx', 'p_ng'], w=['gmx'])
                    S.op('vector', lambda e: e.scalar_tensor_tensor(out=der[:, Q_GMC:Q_GMC + 8], in0=der[:, Q_MODC + 8:Q_MODC + 16], scalar=1.0, in1=par[:, P_NG:P_NG + 8],
                                                                    op0=ALU.add, op1=ALU.mult), r=['modc', 'p_ng'], w=['gmc'])

                def gx_stage():
                    def gx_mm(e):
                        i = None
                        for hf in range(2):
                            for k in range(8):
                                i = e.matmul(bank(5 + hf), lhsT=csb[:, k * 128:(k + 1) * 128], rhs=adaW[:, k * 3072 + 2048 + hf * 512:k * 3072 + 2048 + (hf + 1) * 512],
                                             start=(k == 0), stop=(k == 7))
                        return i
                    S.op('tensor', gx_mm, r=[('adaW', 2)] + [('csb', k) for k in range(8)], w=[('ps', 5), ('ps', 6)])
                    S.op('vector', lambda e: e.tensor_tensor(out=gxb[:], in0=ps[:, 5 * 512:7 * 512], in1=bgrep[:], op=ALU.add),
                         r=[('ps', 5), ('ps', 6), 'bgrep'], w=['gxb'])

                def stA(t):
                    b4 = t % 4
                    src = ctx_d[t * 128:(t + 1) * 128, :] if t < 2 else x_d[(t - 2) * 128:(t - 1) * 128, :]
                    S.op('sync', lambda e: e.dma_start(out=xin[:, b4 * D:(b4 + 1) * D], in_=src), w=[('xin', b4)], dma=('xl', b4))
                    S.op('scalar', lambda e: e.activation(out=junk[:], in_=xin[:, b4 * D:(b4 + 1) * D], func=AF.Square, accum_out=stat[:, t:t + 1]),
                         r=[('xin', b4)], w=['junk', ('ssq', t)])

                def stB(t):
                    S.op('vector', lambda e: e.tensor_scalar(out=stat[:, 24 + t:25 + t], in0=stat[:, t:t + 1], scalar1=1.0 / D, scalar2=1e-6, op0=ALU.mult, op1=ALU.add),
                         r=[('ssq', t)], w=[('ms', t)])
                    S.op('gpsimd', lambda e: e.tensor_tensor(out=stat[:, 48 + t:49 + t], in0=stat[:, 24 + t:25 + t], in1=der[:, Q_NEGH:Q_NEGH + 1], op=ALU.pow),
                         r=[('ms', t), 'negh'], w=[('rstd', t)])

                gcount = [0]

                def stC(t):
                    b4 = t % 4
                    b3 = t % 2
                    gi = t // 2
                    tt = t % 2
                    S.op('vector', lambda e: e.tensor_scalar(out=xs[:, b3 * D:(b3 + 1) * D], in0=xin[:, b4 * D:(b4 + 1) * D], scalar1=stat[:, 48 + t:49 + t], scalar2=None, op0=ALU.mult),
                         r=[('xin', b4), ('rstd', t)], w=[('xs', b3)])

                    def tr_fn(e):
                        i = None
                        for k in range(8):
                            c0 = (k // 2) * 512 + (k % 2) * 256 + tt * 128
                            i = e.transpose(out=ps[:, c0:c0 + 128], in_=xs[:, b3 * D + k * 128:b3 * D + (k + 1) * 128], identity=ident[:])
                        return i
                    S.op('tensor', tr_fn, r=[('xs', b3), 'ident'], w=[('ps', 0), ('ps', 1), ('ps', 2), ('ps', 3)])
                    if tt == 1:
                        if gi == 0:
                            ada_stage()
                        gmo, sho = (Q_GMC, Q_MODC) if gi == 0 else (Q_GMX, Q_MODX)
                        gmn, shn = ('gmc', 'modc') if gi == 0 else ('gmx', 'modx')
                        for k in range(8):
                            c0 = (k // 2) * 512 + (k % 2) * 256
                            dst = xmT[:, k * TT + gi * 256:k * TT + (gi + 1) * 256]
                            if k < 4:
                                S.op('scalar', lambda e, c0=c0, dst=dst, k=k: e.activation(
                                    out=dst, in_=ps[:, c0:c0 + 256], func=AF.Identity, scale=der[:, gmo + k:gmo + k + 1], bias=der[:, sho + k:sho + k + 1]),
                                    r=[('ps', k // 2), gmn, shn], w=[('xmT', gi, k)])
                            else:
                                S.op('vector', lambda e, c0=c0, dst=dst, k=k: e.tensor_scalar(
                                    out=dst, in0=ps[:, c0:c0 + 256], scalar1=der[:, gmo + k:gmo + k + 1], scalar2=der[:, sho + k:sho + k + 1], op0=ALU.mult, op1=ALU.add),
                                    r=[('ps', k // 2), gmn, shn], w=[('xmT', gi, k)])
                        if gi == 8:
                            gx_stage()
                        gjobs = []
                        if gi >= 3:
                            gjobs.append(((gi - 3) // 2, range(0, 4) if gi % 2 == 1 else range(4, 8)))
                        if gi == 8:
                            gjobs.append((3, range(8)))
                        for (j, heads) in gjobs:
                            if j == 3:
                                stPa(0)
                                stPc(0, 0)
                                stPc(0, 1)
                            for h in heads:
                                if j == 3 and h == 4:
                                    aw = [('adaW', 0), ('adaW', 1), ('adaW', 2)]
                                    stPa(1)
                                    stPc(1, 0, aw)
                                    stPc(1, 1, aw)
                                bk = 4 + gcount[0] % 4
                                gcount[0] += 1

                                def g_mm(e, h=h, j=j, bk=bk):
                                    i = None
                                    for k in range(8):
                                        i = e.matmul(bank(bk), lhsT=Wgv[:, k * 1024 + h * 128:k * 1024 + (h + 1) * 128],
                                                     rhs=xmT[:, k * TT + C + j * 512:k * TT + C + (j + 1) * 512], start=(k == 0), stop=(k == 7))
                                    return i
                                S.op('tensor', g_mm, r=['Wgv'] + [('xmT', g2, k) for g2 in (2 * j + 1, 2 * j + 2) for k in range(8)], w=[('ps', bk)])
                                S.op('scalar', lambda e, h=h, j=j, bk=bk: e.activation(out=Ylru[:, h * T + j * 512:h * T + (j + 1) * 512], in_=bank(bk), func=AF.Silu),
                                     r=[('ps', bk)], w=[('Y', h, j)])

                for s_ in range(18 + 2):
                    if s_ == 13:
                        late_loads()
                    if s_ < 18:
                        stA(s_)
                    if 0 <= s_ - 1 < 18:
                        stB(s_ - 1)
                    if 0 <= s_ - 2 < 18:
                        stC(s_ - 2)
            S.barrier()

            with contextlib.ExitStack() as _es:
                xcB = _es.enter_context(nc.sbuf_tensor("xcB", [128, NB], F32))
                xcbB = _es.enter_context(nc.sbuf_tensor("xcbB", [128, NB], BF16))
                xcs[1] = xcB
                xcbs[1] = xcbB
                trb = _es.enter_context(nc.sbuf_tensor("trb", [128, 3 * NB], F32))
                tib = _es.enter_context(nc.sbuf_tensor("tib", [128, 3 * NB], F32))
                ab = _es.enter_context(nc.sbuf_tensor("ab", [128, 3 * NB], F32))
                gpc = [0]

                def stG(h, d, si):
                    hp = h % 2
                    xo = hp * NB
                    do = si * NB
                    col = d * 8 + h
                    for gate, dstb, hbo, nm in ((0, trb, Q_HBA, 'tr'), (1, tib, Q_HBX, 'ti')):
                        woff = ((gate * 2 + d) * 8 + h) * 128
                        for (c0, c1) in ((0, C), (C, C + 1024), (C + 1024, NB)):
                            pp = gpc[0] % 3
                            gpc[0] += 1
                            n = c1 - c0

                            def gate_mm(e, woff=woff, c0=c0, c1=c1, pp=pp):
                                i = None
                                for q0 in range(c0, c1, 512):
                                    q1 = min(q0 + 512, c1)
                                    i = e.matmul(ps[:, pp * 1024 + (q0 - c0):pp * 1024 + (q1 - c0)], lhsT=Wg[:, woff:woff + 128], rhs=xcbs[hp][:, q0:q1],
                                                 start=True, stop=True)
                                return i
                            S.op('tensor', gate_mm, r=[('Wg', 0), ('Wg', 1), ('xcb', hp)], w=[('ps', 2 * pp), ('ps', 2 * pp + 1)])
                            S.op('scalar', lambda e, dstb=dstb, c0=c0, c1=c1, pp=pp, n=n, hbo=hbo: e.activation(
                                out=dstb[:, do + c0:do + c1], in_=ps[:, pp * 1024:pp * 1024 + n], func=AF.Tanh, scale=0.5, bias=der[:, hbo + col:hbo + col + 1]),
                                r=[('ps', 2 * pp), ('ps', 2 * pp + 1), 'hba', 'hbx'], w=[(nm, si)])
                    S.op('scalar', lambda e: e.activation(out=ab[:, do:do + NB], in_=trb[:, do:do + NB], func=AF.Exp,
                                                          scale=der[:, Q_CH + col:Q_CH + col + 1], bias=der[:, Q_CH + col:Q_CH + col + 1]),
                         r=[('tr', si), 'ch'], w=[('a', si)])
                    S.op('scalar', lambda e: e.activation(out=trb[:, do:do + NB], in_=trb[:, do:do + NB], func=AF.Exp,
                                                          scale=der[:, Q_CL + col:Q_CL + col + 1], bias=der[:, Q_CL + col:Q_CL + col + 1]),
                         r=[('tr', si), 'cl'], w=[('tr', si)])
                    S.op('scalar', lambda e: e.activation(out=trb[:, do:do + NB], in_=trb[:, do:do + NB], func=AF.Sqrt, scale=-0.25, bias=0.25),
                         r=[('tr', si)], w=[('tr', si)])

                def stD1(h, d, si):
                    hp = h % 2
                    xo = hp * NB
                    do = si * NB
                    S.op('vector', lambda e: e.scalar_tensor_tensor(out=tib[:, do:do + NB], in0=tib[:, do:do + NB], scalar=1.0, in1=xcs[hp][:], op0=ALU.add, op1=ALU.mult),
                         r=[('ti', si), ('xc', hp, 0), ('xc', hp, C)], w=[('ti', si)])
                    S.op('vector', lambda e: e.tensor_tensor(out=tib[:, do:do + NB], in0=tib[:, do:do + NB], in1=trb[:, do:do + NB], op=ALU.mult),
                         r=[('ti', si), ('tr', si)], w=[('ti', si)])

                def stD2(h, d, si, s0):
                    do = si * NB
                    if d == 0:
                        S.op('vector', lambda e: e.tensor_tensor_scan(out=trb[:, do:do + NB], data0=ab[:, do:do + NB], data1=tib[:, do:do + NB], initial=0.0, op0=ALU.mult, op1=ALU.add),
                             r=[('a', si), ('ti', si)], w=[('tr', si)])
                    else:
                        S.op('vector', lambda e: e.tensor_tensor_scan(out=trb[:, do:do + C][:, ::-1], data0=ab[:, do:do + C][:, ::-1], data1=tib[:, do:do + C][:, ::-1],
                                                                      initial=0.0, op0=ALU.mult, op1=ALU.add), r=[('a', si), ('ti', si)], w=[('tr', si)])
                        S.op('vector', lambda e: e.tensor_tensor_scan(out=trb[:, do + C:do + NB][:, ::-1], data0=ab[:, do + C:do + NB][:, ::-1], data1=tib[:, do + C:do + NB][:, ::-1],
                                                                      initial=trb[:, do:do + 1], op0=ALU.mult, op1=ALU.add), r=[('a', si), ('ti', si), ('tr', si)], w=[('tr', si)])
                        d0 = s0 * NB
                        S.op('vector', lambda e: e.tensor_tensor(out=ab[:, do + C:do + NB], in0=trb[:, d0 + C:d0 + NB], in1=trb[:, do + C:do + NB], op=ALU.add),
                             r=[('tr', s0), ('tr', si), ('a', si)], w=[('a', si)])
                        S.op('vector', lambda e: e.tensor_tensor(out=Ylru[:, h * T:(h + 1) * T], in0=ab[:, do + C:do + NB], in1=Ylru[:, h * T:(h + 1) * T], op=ALU.mult),
                             r=[('a', si)], w=[('Yf', h)])

                units = [(h, d) for h in range(8) for d in range(2)]
                for s_ in range(17):
                    if s_ >= 1:
                        ph, pd = units[s_ - 1]
                        if pd == 1 and ph + 2 < 8:
                            stPa(ph + 2)
                    if s_ < 16:
                        stG(units[s_][0], units[s_][1], s_ % 3)
                    if s_ >= 1:
                        stD1(ph, pd, (s_ - 1) % 3)
                        stD2(ph, pd, (s_ - 1) % 3, (s_ - 2) % 3)
                        if pd == 1 and ph + 2 < 8:
                            stPc(ph + 2, 0)
                        if pd == 0 and 1 <= ph and ph + 1 < 8:
                            stPc(ph + 1, 1)
                    if s_ == 15:
                        S.op('gpsimd', lambda e: e.dma_start(out=k3(late['Wv'][:], 1024), in_=winv[:, :, 3072:4096]),
                             w=[('Wg', 0), ('Wg', 1), ('Wxs', 0), ('Wxs', 1), 'xa_c', 'xa_x'], dma=('wc', 'Wv'))
                        S.op('gpsimd', lambda e: e.dma_start(out=k3(late['Wu'][:], 1024), in_=winv[:, :, 2048:3072]),
                             w=['xa_x', ('xc', 0, 0), ('xc', 0, C), ('xcb', 0)], dma=('wc', 'Wu'))
            S.barrier()

        with contextlib.ExitStack() as _es:
            Wv = _es.enter_context(nc.sbuf_tensor("Wv", [128, 8 * 1024], BF16))
            late['Wv'] = Wv
            Wu = _es.enter_context(nc.sbuf_tensor("Wu", [128, 8 * 1024], BF16))
            late['Wu'] = Wu
            Wgb = _es.enter_context(nc.sbuf_tensor("Wgb", [128, 8 * 1024], BF16))
            wob = _es.enter_context(nc.sbuf_tensor("wob", [128, 16 * 1024], BF16))
            fgb = _es.enter_context(nc.sbuf_tensor("fgb", [128, D], F32))
            Bt = _es.enter_context(nc.sbuf_tensor("Bt", [128, 8 * 128], F32))
            wsTb = _es.enter_context(nc.sbuf_tensor("wsTb", [128, 1024], BF16))
            gv = _es.enter_context(nc.sbuf_tensor("gv", [128, D], F32))
            vhat = _es.enter_context(nc.sbuf_tensor("vhat", [128, 2 * D], BF16))
            gu = _es.enter_context(nc.sbuf_tensor("gu", [128, 8 * 256], F32))
            sgb = _es.enter_context(nc.sbuf_tensor("sgb", [128, 8 * 256], F32))
            Ysgu = _es.enter_context(nc.sbuf_tensor("Ysgu", [128, 8 * 256], BF16))
            xres = _es.enter_context(nc.sbuf_tensor("xres", [128, D], F32))
            xn = _es.enter_context(nc.sbuf_tensor("xn", [128, 2 * D], F32))
            st3 = _es.enter_context(nc.sbuf_tensor("st3", [128, 96], F32))
            S.op('gpsimd', lambda e: e.dma_start(out=wsTb[:], in_=wsT_d[:, :]), w=['wsTb'], dma=('wc', 'wsTb'))
            S.op('gpsimd', lambda e: e.dma_start(out=k3(Wgb[:], 1024), in_=winv[:, :, 4096:5120]), w=['Wgb'], dma=('wc', 'Wgb'))
            for hh in range(2):
                S.op('gpsimd', lambda e, hh=hh: e.dma_start(out=k3(wob[:, hh * 8192:(hh + 1) * 8192], 1024), in_=woutv[:, hh * 8:(hh + 1) * 8, :]),
                     w=[('wob', hh)], dma=('wc', 'wob', hh))
            ld(fgb[:], fgrep_d[:, :], ['fgb'])
            ld(xn[:, 0:D], sbrep_d[:, :], [('xn', 0)])

            def st_bt():
                def rs_mm(e):
                    i = None
                    for hf in range(2):
                        i = e.matmul(bank(6 + hf), lhsT=ones_b[:], rhs=wsTb[:, hf * 512:(hf + 1) * 512], start=True, stop=True)
                    return i
                S.op('tensor', rs_mm, r=['ones_b', 'wsTb'], w=[('ps', 6), ('ps', 7)])
                for g in range(8):
                    S.op('vector', lambda e, g=g: e.scalar_tensor_tensor(
                        out=Bt[:, g * 128:(g + 1) * 128], in0=ps[:, 6 * 512 + g * 128:6 * 512 + (g + 1) * 128],
                        scalar=par[:, P_LNB + g:P_LNB + g + 1], in1=xn[:, g * 128:(g + 1) * 128], op0=ALU.mult, op1=ALU.add),
                        r=[('ps', 6), ('ps', 7), ('xn', 0)], w=[('Bt', g)])

            ci = [0]

            def st_v(j, n):
                X0 = C + j * 256
                if True:
                    ci[0] += 1
                    so = (ci[0] % 4) * 16

                    def v_mm(e, n=n):
                        i = None
                        for hf in range(2):
                            for k in range(8):
                                i = e.matmul(bank(hf), lhsT=xmT[:, k * TT + X0 + n * 128:k * TT + X0 + (n + 1) * 128],
                                             rhs=Wv[:, k * 1024 + hf * 512:k * 1024 + (hf + 1) * 512], start=(k == 0), stop=(k == 7))
                        return i
                    S.op('tensor', v_mm, r=['Wv'], w=[('ps', 0), ('ps', 1)])
                    S.op('scalar', lambda e: e.activation(out=gv[:], in_=ps[:, 0:1024], func=AF.Gelu_apprx_tanh), r=[('ps', 0), ('ps', 1)], w=['gv'])
                    for hf in range(2):
                        S.op('vector', lambda e, hf=hf, so=so: e.bn_stats(out=st3[:, so + hf * 6:so + hf * 6 + 6], in_=gv[:, hf * 512:(hf + 1) * 512]),
                             r=['gv'], w=[('bst', so, hf)])
                    S.op('vector', lambda e, so=so: e.bn_aggr(out=st3[:, so + 12:so + 14], in_=st3[:, so:so + 12]), r=[('bst', so, 0), ('bst', so, 1)], w=[('mv', so)])
                    S.op('vector', lambda e, so=so: e.tensor_scalar(out=st3[:, so + 14:so + 15], in0=st3[:, so + 13:so + 14], scalar1=1e-5, scalar2=None, op0=ALU.add),
                         r=[('mv', so)], w=[('ve', so)])
                    S.op('gpsimd', lambda e, so=so: e.tensor_tensor(out=st3[:, so + 15:so + 16], in0=st3[:, so + 14:so + 15], in1=der[:, Q_NEGH:Q_NEGH + 1], op=ALU.pow),
                         r=[('ve', so)], w=[('rs', so)])
                    S.op('vector', lambda e, so=so, n=n: e.tensor_scalar(out=vhat[:, n * D:(n + 1) * D], in0=gv[:], scalar1=st3[:, so + 12:so + 13],
                                                                          scalar2=st3[:, so + 15:so + 16], op0=ALU.subtract, op1=ALU.mult),
                         r=['gv', ('mv', so), ('rs', so)], w=[('vhat', n)])

            def st_ug(j, which, pairs):
                X0 = C + j * 256
                for (Wt, wn, dst, dn, fn_) in (((Wu, 'Wu', gu, 'gu', AF.Gelu_apprx_tanh), (Wgb, 'Wgb', sgb, 'sgb', AF.Silu))[which],):
                    for gpair in pairs:
                        bk = 4 + gpair % 2

                        def ug_mm(e, gpair=gpair, bk=bk, Wt=Wt):
                            i = None
                            for g in (2 * gpair, 2 * gpair + 1):
                                co = (g % 2) * 256
                                for k in range(8):
                                    i = e.matmul(bank(bk, co, co + 256), lhsT=Wt[:, k * 1024 + g * 128:k * 1024 + (g + 1) * 128], rhs=xmT[:, k * TT + X0:k * TT + X0 + 256],
                                                 start=(k == 0), stop=(k == 7), skip_group_check=True)
                            return i
                        S.op('tensor', ug_mm, r=[wn], w=[('ps', bk)])
                        S.op('scalar', lambda e, gpair=gpair, bk=bk, dst=dst, fn_=fn_: e.activation(out=dst[:, gpair * 512:(gpair + 1) * 512], in_=bank(bk), func=fn_),
                             r=[('ps', bk)], w=[(dn, gpair)])

            def st_gg(j):
                S.op('vector', lambda e: e.tensor_tensor(out=gu[:], in0=gu[:], in1=sgb[:], op=ALU.mult),
                     r=[('gu', g) for g in range(4)] + [('sgb', g) for g in range(4)], w=[('gu', g) for g in range(4)])

            def st_mix(j, pairs, final):
                for gpair in pairs:
                    bk = 6 + gpair % 2

                    def mix_mm(e, gpair=gpair, bk=bk):
                        i = None
                        for g in (2 * gpair, 2 * gpair + 1):
                            co = (g % 2) * 256
                            for n in range(2):
                                i = e.matmul(bank(bk, co + n * 128, co + (n + 1) * 128), lhsT=vhat[:, n * D + g * 128:n * D + (g + 1) * 128], rhs=wsTb[:, g * 128:(g + 1) * 128],
                                             start=True, stop=True, skip_group_check=True)
                        return i
                    S.op('tensor', mix_mm, r=[('vhat', 0), ('vhat', 1), 'wsTb'], w=[('ps', bk)])
                    for g in (2 * gpair, 2 * gpair + 1):
                        co = (g % 2) * 256
                        for n in range(2):
                            S.op('vector', lambda e, g=g, bk=bk, co=co, n=n: e.scalar_tensor_tensor(
                                out=sgb[:, g * 256 + n * 128:g * 256 + (n + 1) * 128], in0=bank(bk, co + n * 128, co + (n + 1) * 128),
                                scalar=par[:, P_LNG + g:P_LNG + g + 1], in1=Bt[:, g * 128:(g + 1) * 128], op0=ALU.mult, op1=ALU.add),
                                r=[('ps', bk), 'p_lng', ('Bt', g)], w=[('sgb', gpair)])
                if final:
                    S.op('vector', lambda e: e.tensor_tensor(out=Ysgu[:], in0=sgb[:], in1=gu[:], op=ALU.mult),
                         r=[('gu', g) for g in range(4)] + [('sgb', g) for g in range(4)], w=['Ysgu'])

            def st_o1(j, n):
                tg = j * 2 + n
                ob = tg % 2
                so = 64 + (tg % 4) * 4
                pb = 2 if n == 0 else 0
                S.op('sync', lambda e: e.dma_start(out=xres[:], in_=x_d[tg * 128:(tg + 1) * 128, :]), w=['xres'], dma=('xr', 0))

                def o_mm(e):
                    i = None
                    for hf in range(2):
                        for kc in range(16):
                            if kc < 8:
                                lt = Ylru[:, kc * T + j * 256 + n * 128:kc * T + j * 256 + (n + 1) * 128]
                            else:
                                lt = Ysgu[:, (kc - 8) * 256 + n * 128:(kc - 8) * 256 + (n + 1) * 128]
                            i = e.matmul(bank(pb + hf), lhsT=lt, rhs=wob[:, kc * 1024 + hf * 512:kc * 1024 + (hf + 1) * 512], start=(kc == 0), stop=(kc == 15))
                    return i
                S.op('tensor', o_mm, r=['Ysgu', ('wob', 0), ('wob', 1)], w=[('ps', pb), ('ps', pb + 1)])
                S.op('vector', lambda e: e.tensor_tensor(out=xn[:, ob * D:(ob + 1) * D], in0=ps[:, pb * 512:(pb + 2) * 512], in1=gxb[:], op=ALU.mult),
                     r=[('ps', pb), ('ps', pb + 1), 'gxb'] + [('Bt', g) for g in range(8)], w=[('xn', ob)])
                S.op('vector', lambda e: e.tensor_tensor(out=xn[:, ob * D:(ob + 1) * D], in0=xn[:, ob * D:(ob + 1) * D], in1=xres[:], op=ALU.add),
                     r=[('xn', ob), 'xres'], w=[('xn', ob)])
                S.op('scalar', lambda e: e.activation(out=xres[:], in_=xn[:, ob * D:(ob + 1) * D], func=AF.Square, accum_out=st3[:, so:so + 1]),
                     r=[('xn', ob)], w=['xres', ('q0', so)])

            def st_o2(j, n):
                tg = j * 2 + n
                so = 64 + (tg % 4) * 4
                S.op('vector', lambda e: e.tensor_scalar(out=st3[:, so + 1:so + 2], in0=st3[:, so:so + 1], scalar1=1.0 / D, scalar2=1e-6, op0=ALU.mult, op1=ALU.add),
                     r=[('q0', so)], w=[('q1', so)])
                S.op('gpsimd', lambda e: e.tensor_tensor(out=st3[:, so + 2:so + 3], in0=st3[:, so + 1:so + 2], in1=der[:, Q_NEGH:Q_NEGH + 1], op=ALU.pow),
                     r=[('q1', so)], w=[('q2', so)])

            def st_o3(j, n):
                tg = j * 2 + n
                ob = tg % 2
                so = 64 + (tg % 4) * 4
                S.op('vector', lambda e: e.scalar_tensor_tensor(out=xn[:, ob * D:(ob + 1) * D], in0=xn[:, ob * D:(ob + 1) * D], scalar=st3[:, so + 2:so + 3],
                                                                in1=fgb[:], op0=ALU.mult, op1=ALU.mult),
                     r=[('xn', ob), ('q2', so), 'fgb'], w=[('xn', ob)])
                S.op('sync', lambda e: e.dma_start(out=out_d[tg * 128:(tg + 1) * 128, :], in_=xn[:, ob * D:(ob + 1) * D]), r=[('xn', ob)], dma=('st', ob))

            for j in range(8):
                st_v(j, 0)
                st_ug(j, 0, (0, 1))
                st_v(j, 1)
                st_ug(j, 0, (2, 3))
                st_ug(j, 1, (0, 1, 2, 3))
                st_gg(j)
                if j == 0:
                    st_bt()
                if j >= 1:
                    st_o1(j - 1, 0)
                st_mix(j, (0, 1), False)
                if j >= 1:
                    st_o1(j - 1, 1)
                st_mix(j, (2, 3), True)
                if j >= 1:
                    st_o2(j - 1, 0)
                    st_o2(j - 1, 1)
                    st_o3(j - 1, 0)
                    st_o3(j - 1, 1)
            for n in range(2):
                st_o1(7, n)
            for n in range(2):
                st_o2(7, n)
            for n in range(2):
                st_o3(7, n)

        keys = list(S.cnt.keys())
        with contextlib.ExitStack() as es:
            sems = {}
            for i, k in enumerate(keys):
                sems[k] = es.enter_context(nc.semaphore("s%d" % i))
            block = es.enter_context(nc.Block())
            S.emit(nc, block, sems)
    return nc


_NC_CACHE = {}


def kernel(x, c, ctx, c_ctx, ada_w, ada_b, norm_g, w_in, conv_w, conv_b, lru_wa, lru_ba, lru_wx, lru_bx,
           lru_lambda, sgu_ln_g, sgu_ln_b, sgu_w, sgu_b, w_out, final_g):
    f = lambda a: np.ascontiguousarray(np.asarray(a, dtype=np.float32))
    x = f(x); c = f(c); ctx = f(ctx); c_ctx = f(c_ctx)
    ada_w0 = f(ada_w[0]); ada_b0 = f(ada_b[0]); ng = f(norm_g[0]); w_in0 = f(w_in[0]); w_out0 = f(w_out[0])

    def colT(v, nchunk):
        return f(np.asarray(v, dtype=np.float32).reshape(nchunk, 128).T)
    adabT = colT(ada_b0, 24)
    bg_rep = f(np.broadcast_to(ada_b0[2048:3072][None, :], (128, 1024)))
    fg_rep = f(np.broadcast_to(np.asarray(final_g, dtype=np.float32)[None, :], (128, 1024)))
    ngT = colT(ng, 8)
    cwT = f(np.asarray(conv_w[0], dtype=np.float32).reshape(4, 8, 128).transpose(2, 1, 0).reshape(128, 32))
    cbT = colT(conv_b[0], 8)
    wa = np.asarray(lru_wa[0], dtype=np.float32)
    wx = np.asarray(lru_wx[0], dtype=np.float32)
    wg = f(np.stack([wa, wx], 0).transpose(3, 0, 1, 2, 4).reshape(128, 4096))
    baT = f(np.asarray(lru_ba[0], dtype=np.float32).transpose(2, 0, 1).reshape(128, 16))
    bxT = f(np.asarray(lru_bx[0], dtype=np.float32).transpose(2, 0, 1).reshape(128, 16))
    lamT = f(np.asarray(lru_lambda[0], dtype=np.float32).reshape(2, 8, 128).transpose(2, 0, 1).reshape(128, 16))
    lngT = colT(sgu_ln_g[0], 8)
    lnbT = colT(sgu_ln_b[0], 8)
    wsT = f(np.asarray(sgu_w[0], dtype=np.float32).transpose(2, 0, 1).reshape(128, 1024))
    sb_rep = f(np.broadcast_to(np.asarray(sgu_b[0], dtype=np.float32).reshape(1, 1024), (128, 1024)))

    if 'nc' not in _NC_CACHE:
        _NC_CACHE['nc'] = build_nc()
    nc = _NC_CACHE['nc']
    in_maps = []
    for b in range(NCORES):
        cT = f(np.stack([c[b].reshape(8, 128).T, c_ctx.reshape(8, 128).T], axis=2).reshape(128, 16))
        in_maps.append({
            "x": x[b], "ctx": ctx[b], "cT": cT, "ada_w": ada_w0, "adabT": adabT, "bg_rep": bg_rep, "fg_rep": fg_rep,
            "ngT": ngT, "w_in": w_in0, "w_out": w_out0, "cwT": cwT, "cbT": cbT, "wg": wg, "baT": baT, "bxT": bxT,
            "lamT": lamT, "lngT": lngT, "lnbT": lnbT, "wsT": wsT, "sb_rep": sb_rep,
        })
    res = run_bass_kernel_spmd(nc, in_maps, core_ids=list(range(NCORES)))
    return np.stack([np.asarray(r["out"], dtype=np.float32) for r in res.results], axis=0)
```

```python
import contextlib
import numpy as np
import concourse.bass as bass
import concourse.mybir as mybir
from concourse.bass_utils import run_bass_kernel_spmd
from concourse.alu_op_type import AluOpType as ALU

F32 = mybir.dt.float32
BF16 = mybir.dt.bfloat16
AF = mybir.ActivationFunctionType

D = 1024
T = 2048
C = 256
TT = T + C
NCORES = 8
ENGS = ['sync', 'scalar', 'vector', 'gpsimd', 'tensor']


class Sched:
    def __init__(self):
        self.all = []
        self.cnt = {}
        self.lastw = {}
        self.readers = {}

    def op(self, eng, fn, r=(), w=(), dma=None):
        key = ('d', dma) if dma else ('c', eng)
        inc = 16 if dma else 1
        self.cnt[key] = self.cnt.get(key, 0) + inc
        tok = (key, self.cnt[key])
        deps = {}

        def add(k, v):
            if deps.get(k, 0) < v:
                deps[k] = v
        xs_ = [s for s in r if isinstance(s, tuple) and s[0] == 'ps']
        r = [s for s in r if not (isinstance(s, tuple) and s[0] == 'ps')]
        for s in r:
            t = self.lastw.get(s)
            if t: add(t[0], t[1])
        for s in w:
            t = self.lastw.get(s)
            if t: add(t[0], t[1])
            for k, v in self.readers.get(s, {}).items():
                add(k, v)
        for s in xs_:
            t = self.lastw.get(s)
            if t and not (t[0] == key and t[2]):
                add(t[0], t[1])
            for k, v in self.readers.get(s, {}).items():
                add(k, v)
        for s in r:
            rd = self.readers.setdefault(s, {})
            if rd.get(key, 0) < tok[1]:
                rd[key] = tok[1]
        for s in w:
            self.lastw[s] = (tok[0], tok[1], False)
            self.readers[s] = {}
        for s in xs_:
            self.lastw[s] = (tok[0], tok[1], True)
            self.readers[s] = {}
        self.all.append((eng, fn, deps, key, inc))
        return tok

    def barrier(self):
        allc = dict(self.cnt)
        for e in ENGS:
            self.all.append((e, None, dict(allc), None, 0))
        self.lastw = {}
        self.readers = {}

    def emit(self, nc, block, sems, limit=None):
        ops = self.all if limit is None else self.all[:limit]
        final = {}
        for (eng, fn, deps, key, inc) in ops:
            if fn is not None:
                final[key] = final.get(key, 0) + inc
        for eng in ENGS:
            q = [o for o in ops if o[0] == eng]

            def body(e, q=q, eng=eng):
                waited = {}
                for (_, fn, deps, key, inc) in q:
                    for k, v in deps.items():
                        if eng == 'tensor' and k == ('c', 'tensor'):
                            continue
                        if waited.get(k, 0) < v:
                            e.wait_ge(sems[k], v)
                            waited[k] = v
                    if fn is not None:
                        ins = fn(e)
                        ins.then_inc(sems[key], inc)
                for k, v in final.items():
                    if waited.get(k, 0) < v:
                        e.wait_ge(sems[k], v)
            getattr(block, eng)(body)


def build_nc():
    nc = bass.Bass("TRN2", target_bir_lowering=False)

    def din(name, shape):
        return nc.dram_tensor(name, list(shape), F32, kind="ExternalInput").ap()
    x_d = din("x", [T, D])
    ctx_d = din("ctx", [C, D])
    cT_d = din("cT", [128, 16])
    adaw_d = din("ada_w", [D, 3 * D])
    adabT_d = din("adabT", [128, 24])
    bgrep_d = din("bg_rep", [128, D])
    fgrep_d = din("fg_rep", [128, D])
    ngT_d = din("ngT", [128, 8])
    win_d = din("w_in", [D, 5 * D])
    wout_d = din("w_out", [2 * D, D])
    cwT_d = din("cwT", [128, 32])
    cbT_d = din("cbT", [128, 8])
    wg_d = din("wg", [128, 4096])
    baT_d = din("baT", [128, 16])
    bxT_d = din("bxT", [128, 16])
    lamT_d = din("lamT", [128, 16])
    lngT_d = din("lngT", [128, 8])
    lnbT_d = din("lnbT", [128, 8])
    wsT_d = din("wsT", [128, 1024])
    sbrep_d = din("sb_rep", [128, 1024])
    out_d = nc.dram_tensor("out", [T, D], F32, kind="ExternalOutput").ap()

    S = Sched()
    late = {}
    winv = win_d.rearrange("(k p) n -> p k n", p=128)
    woutv = wout_d.rearrange("(k p) n -> p k n", p=128)
    adawv = adaw_d.rearrange("(k p) n -> p k n", p=128)

    P_CT, P_ADAB, P_NG, P_CW, P_CB, P_BA, P_BX, P_LAM, P_LNG, P_LNB = 0, 16, 40, 48, 80, 88, 104, 120, 136, 144
    NPAR = 152
    Q_CS, Q_MODX, Q_MODC, Q_GMX, Q_GMC, Q_E1, Q_SP, Q_CL, Q_CH, Q_HBA, Q_HBX, Q_NEGH = 0, 16, 32, 48, 56, 64, 80, 96, 112, 128, 144, 160
    NDER = 168
    NB = TT

    def k3(t, n):
        return t.rearrange("p (k n) -> p k n", n=n)

    with contextlib.ExitStack() as _es:
        par = _es.enter_context(nc.sbuf_tensor("par", [128, NPAR], F32))
        der = _es.enter_context(nc.sbuf_tensor("der", [128, NDER], F32))
        stat = _es.enter_context(nc.sbuf_tensor("stat", [128, 4 * 24], F32))
        ident = _es.enter_context(nc.sbuf_tensor("ident", [128, 128], F32))
        ones_b = _es.enter_context(nc.sbuf_tensor("ones_b", [128, 128], BF16))
        csbf = _es.enter_context(nc.sbuf_tensor("csbf", [128, 16], BF16))
        gxb = _es.enter_context(nc.sbuf_tensor("gxb", [128, D], F32))
        xmT = _es.enter_context(nc.sbuf_tensor("xmT", [128, 8 * TT], BF16))
        Ylru = _es.enter_context(nc.sbuf_tensor("Ylru", [128, 8 * T], BF16))
        ps = _es.enter_context(nc.psum_tensor("ps", [128, 4096], F32))

        def bank(i, c0=0, c1=512):
            return ps[:, i * 512 + c0:i * 512 + c1]

        def ld(dst, src, w, r=()):
            S.op('sync', lambda e: e.dma_start(out=dst, in_=src), r=r, w=w, dma=('par', w[0]))

        with contextlib.ExitStack() as _es:
            Wg = _es.enter_context(nc.sbuf_tensor("Wg", [128, 4096], BF16))
            Wxs = _es.enter_context(nc.sbuf_tensor("Wxs", [128, 2 * 1024], BF16))
            xa_c = _es.enter_context(nc.sbuf_tensor("xa_c", [128, C + 3], F32))
            xa_x = _es.enter_context(nc.sbuf_tensor("xa_x", [128, T + 3], F32))
            xcA = _es.enter_context(nc.sbuf_tensor("xcA", [128, NB], F32))
            xcbA = _es.enter_context(nc.sbuf_tensor("xcbA", [128, NB], BF16))
            ibc = [0]
            def wxa_load(h):
                S.op('gpsimd', lambda e, h=h: e.dma_start(out=k3(Wxs[:, (h % 2) * 1024:(h % 2 + 1) * 1024], 128), in_=winv[:, :, h * 128:(h + 1) * 128]),
                     w=[('Wxs', h % 2)], dma=('wc', 'Wxs', h % 2))

            xcs = [xcA, None]
            xcbs = [xcbA, None]

            def stPa(h):
                hp = h % 2
                for j in range(5):
                    c0, c1 = (0, C) if j == 0 else (C + (j - 1) * 512, C + j * 512)
                    n = c1 - c0
                    bk = 6 + (ibc[0] % 2)
                    ibc[0] += 1

                    def xa_mm(e, c0=c0, c1=c1, n=n, bk=bk):
                        i = None
                        for k in range(8):
                            i = e.matmul(bank(bk, 0, n), lhsT=Wxs[:, hp * 1024 + k * 128:hp * 1024 + (k + 1) * 128],
                                         rhs=xmT[:, k * TT + c0:k * TT + c1], start=(k == 0), stop=(k == 7))
                        return i
                    S.op('tensor', xa_mm, r=[('Wxs', hp)] + [('xmT', g2, k) for g2 in range(c0 // 256, (c1 + 255) // 256) for k in range(8)], w=[('ps', bk)])
                    if j == 0:
                        S.op('scalar', lambda e, bk=bk: e.activation(out=xa_c[:, 1:1 + C], in_=bank(bk, 0, C), func=AF.Identity), r=[('ps', bk)], w=['xa_c'])
                    else:
                        S.op('scalar', lambda e, bk=bk, j=j: e.activation(out=xa_x[:, 1 + (j - 1) * 512:1 + j * 512], in_=bank(bk), func=AF.Identity), r=[('ps', bk)], w=['xa_x'])
                if h + 2 < 8:
                    wxa_load(h + 2)

            def stPc(h, part):
                hp = h % 2
                xo = hp * NB
                (src, sl, L, o0) = ((xa_c, 'xa_c', C, 0), (xa_x, 'xa_x', T, C))[part]
                S.op('vector', lambda e: e.tensor_scalar(
                    out=xcs[hp][:, o0:o0 + L], in0=src[:, 0:L], scalar1=par[:, P_CW + h * 4:P_CW + h * 4 + 1], scalar2=par[:, P_CB + h:P_CB + h + 1],
                    op0=ALU.mult, op1=ALU.add), r=[sl, 'p_cw', 'p_cb'], w=[('xc', hp, o0)])
                for tp in range(1, 4):
                    S.op('vector', lambda e, tp=tp: e.scalar_tensor_tensor(
                        out=xcs[hp][:, o0:o0 + L], in0=src[:, tp:tp + L], scalar=par[:, P_CW + h * 4 + tp:P_CW + h * 4 + tp + 1], in1=xcs[hp][:, o0:o0 + L],
                        op0=ALU.mult, op1=ALU.add), r=[sl, 'p_cw', ('xc', hp, o0)], w=[('xc', hp, o0)])
                if part == 1:
                    S.op('vector', lambda e: e.tensor_copy(out=xcbs[hp][:], in_=xcs[hp][:]), r=[('xc', hp, 0), ('xc', hp, C)], w=[('xcb', hp)])


            with contextlib.ExitStack() as _es:
                adaW = _es.enter_context(nc.sbuf_tensor("adaW", [128, 8 * 3072], BF16))
                Wgv = _es.enter_context(nc.sbuf_tensor("Wga", [128, 8 * 1024], BF16))
                xin = _es.enter_context(nc.sbuf_tensor("xin", [128, 4 * D], F32))
                xs = _es.enter_context(nc.sbuf_tensor("xs", [128, 2 * D], F32))
                junk = _es.enter_context(nc.sbuf_tensor("junk", [128, D], BF16))
                csb = _es.enter_context(nc.sbuf_tensor("csb", [128, 8 * 128], BF16))
                bgrep = _es.enter_context(nc.sbuf_tensor("bgrep", [128, 1024], F32))
                ld(par[:, P_CT:P_CT + 16], cT_d[:, :], ['p_ct'])
                ld(par[:, P_ADAB:P_ADAB + 24], adabT_d[:, :], ['p_adab'])
                ld(par[:, P_NG:P_NG + 8], ngT_d[:, :], ['p_ng'])
                for sl in range(2):
                    S.op('gpsimd', lambda e, sl=sl: e.dma_start(out=k3(adaW[:], 3072)[:, :, sl * 1024:(sl + 1) * 1024], in_=adawv[:, :, sl * 1024:(sl + 1) * 1024]),
                         w=[('adaW', sl)], dma=('wc', 'adaW', sl))
                S.op('gpsimd', lambda e: e.dma_start(out=k3(Wgv[:], 1024), in_=winv[:, :, 1024:2048]), w=['Wgv'], dma=('wc', 'Wgv'))

                def late_loads():
                    S.op('gpsimd', lambda e: e.dma_start(out=k3(adaW[:], 3072)[:, :, 2048:3072], in_=adawv[:, :, 2048:3072]), w=[('adaW', 2)], dma=('wc', 'adaW', 2))
                    for hh in range(2):
                        S.op('gpsimd', lambda e, hh=hh: e.dma_start(out=Wg[:, hh * 2048:(hh + 1) * 2048], in_=wg_d[:, hh * 2048:(hh + 1) * 2048]),
                             w=[('Wg', hh)], dma=('wc', 'Wg', hh))
                    wxa_load(0)
                    wxa_load(1)
                ld(par[:, P_LAM:P_LAM + 16], lamT_d[:, :], ['p_lam'])
                ld(par[:, P_BA:P_BA + 16], baT_d[:, :], ['p_ba'])
                ld(par[:, P_BX:P_BX + 16], bxT_d[:, :], ['p_bx'])
                ld(par[:, P_CW:P_CW + 32], cwT_d[:, :], ['p_cw'])
                ld(par[:, P_CB:P_CB + 8], cbT_d[:, :], ['p_cb'])
                ld(par[:, P_LNG:P_LNG + 8], lngT_d[:, :], ['p_lng'])
                ld(par[:, P_LNB:P_LNB + 8], lnbT_d[:, :], ['p_lnb'])
                ld(bgrep[:], bgrep_d[:, :], ['bgrep'])

                S.op('gpsimd', lambda e: e.memset(ident[:], 0.0), w=['ident'])
                S.op('gpsimd', lambda e: e.affine_select(out=ident[:], in_=ident[:], pattern=[[-1, 128]], compare_op=ALU.not_equal,
                                                         fill=1.0, base=0, channel_multiplier=1), r=['ident'], w=['ident'])
                S.op('gpsimd', lambda e: e.memset(ones_b[:], 1.0), w=['ones_b'])
                S.op('gpsimd', lambda e: e.memset(xa_c[:], 0.0), w=['xa_c'])
                S.op('gpsimd', lambda e: e.memset(xa_x[:], 0.0), w=['xa_x'])
                S.op('gpsimd', lambda e: e.memset(der[:, Q_NEGH:Q_NEGH + 1], -0.5), w=['negh'])

                S.op('scalar', lambda e: e.activation(out=csbf[:], in_=par[:, P_CT:P_CT + 16], func=AF.Silu), r=['p_ct'], w=['cs'])
                S.op('scalar', lambda e: e.activation(out=der[:, Q_E1:Q_E1 + 16], in_=par[:, P_LAM:P_LAM + 16], func=AF.Exp, scale=-1.0), r=['p_lam'], w=['e1'])
                S.op('scalar', lambda e: e.activation(out=der[:, Q_SP:Q_SP + 16], in_=der[:, Q_E1:Q_E1 + 16], func=AF.Ln, bias=1.0, scale=1.0), r=['e1'], w=['sp'])
                S.op('vector', lambda e: e.tensor_scalar(out=der[:, Q_CL:Q_CL + 16], in0=der[:, Q_SP:Q_SP + 16], scalar1=-8.0, scalar2=None, op0=ALU.mult), r=['sp'], w=['cl'])
                S.op('vector', lambda e: e.tensor_scalar(out=der[:, Q_CH:Q_CH + 16], in0=der[:, Q_SP:Q_SP + 16], scalar1=-4.0, scalar2=None, op0=ALU.mult), r=['sp'], w=['ch'])
                S.op('vector', lambda e: e.tensor_scalar(out=der[:, Q_HBA:Q_HBA + 16], in0=par[:, P_BA:P_BA + 16], scalar1=0.5, scalar2=None, op0=ALU.mult), r=['p_ba'], w=['hba'])
                S.op('vector', lambda e: e.tensor_scalar(out=der[:, Q_HBX:Q_HBX + 16], in0=par[:, P_BX:P_BX + 16], scalar1=0.5, scalar2=None, op0=ALU.mult), r=['p_bx'], w=['hbx'])
                for k in range(8):
                    S.op('vector', lambda e, k=k: e.tensor_scalar(out=csb[:, k * 128:(k + 1) * 128], in0=ones_b[:], scalar1=csbf[:, 2 * k:2 * k + 1],
                                                                  scalar2=None, op0=ALU.mult), r=['ones_b', 'cs'], w=[('csb', k)])

                def ada_stage():
                    def ada_mm(e):
                        i = None
                        for fc in range(16):
                            for k in range(8):
                                i = e.matmul(ps[:, 7 * 512 + fc * 2:7 * 512 + fc * 2 + 2], lhsT=adaW[:, k * 3072 + fc * 128:k * 3072 + (fc + 1) * 128],
                                             rhs=csbf[:, 2 * k:2 * k + 2], start=(k == 0), stop=(k == 7), skip_group_check=True)
                        return i
                    S.op('tensor', ada_mm, r=[('adaW', 0), ('adaW', 1), 'cs'], w=[('ps', 7)])
                    modps = ps[:, 7 * 512:7 * 512 + 32].rearrange("p (f v) -> p f v", v=2)
                    S.op('vector', lambda e: e.tensor_tensor(out=der[:, Q_MODX:Q_MODX + 16], in0=modps[:, :, 0], in1=par[:, P_ADAB:P_ADAB + 16], op=ALU.add),
                         r=[('ps', 7), 'p_adab'], w=['modx'])
                    S.op('vector', lambda e: e.tensor_tensor(out=der[:, Q_MODC:Q_MODC + 16], in0=modps[:, :, 1], in1=par[:, P_ADAB:P_ADAB + 16], op=ALU.add),
                         r=[('ps', 7), 'p_adab'], w=['modc'])
                    S.op('vector', lambda e: e.scalar_tensor_tensor(out=der[:, Q_GMX:Q_GMX + 8], in0=der[:, Q_MODX + 8:Q_MODX + 16], scalar=1.0, in1=par[:, P_NG:P_NG + 8],
                                                                    op0=ALU.add, op1=ALU.mult), r=['modx', 'p_ng'], w=['gmx'])
                    S.op('vector', lambda e: e.scalar_tensor_tensor(out=der[:, Q_GMC:Q_GMC + 8], in0=der[:, Q_MODC + 8:Q_MODC + 16], scalar=1.0, in1=par[:, P_NG:P_NG + 8],
                                                                    op0=ALU.add, op1=ALU.mult), r=['modc', 'p_ng'], w=['gmc'])

                def gx_stage():
                    def gx_mm(e):
                        i = None
                        for hf in range(2):
                            for k in range(8):
                                i = e.matmul(bank(5 + hf), lhsT=csb[:, k * 128:(k + 1) * 128], rhs=adaW[:, k * 3072 + 2048 + hf * 512:k * 3072 + 2048 + (hf + 1) * 512],
                                             start=(k == 0), stop=(k == 7))
                        return i
                    S.op('tensor', gx_mm, r=[('adaW', 2)] + [('csb', k) for k in range(8)], w=[('ps', 5), ('ps', 6)])
                    S.op('vector', lambda e: e.tensor_tensor(out=gxb[:], in0=ps[:, 5 * 512:7 * 512], in1=bgrep[:], op=ALU.add),
                         r=[('ps', 5), ('ps', 6), 'bgrep'], w=['gxb'])

                def stA(t):
                    b4 = t % 4
                    src = ctx_d[t * 128:(t + 1) * 128, :] if t < 2 else x_d[(t - 2) * 128:(t - 1) * 128, :]
                    S.op('sync', lambda e: e.dma_start(out=xin[:, b4 * D:(b4 + 1) * D], in_=src), w=[('xin', b4)], dma=('xl', b4))
                    S.op('scalar', lambda e: e.activation(out=junk[:], in_=xin[:, b4 * D:(b4 + 1) * D], func=AF.Square, accum_out=stat[:, t:t + 1]),
                         r=[('xin', b4)], w=['junk', ('ssq', t)])

                def stB(t):
                    S.op('vector', lambda e: e.tensor_scalar(out=stat[:, 24 + t:25 + t], in0=stat[:, t:t + 1], scalar1=1.0 / D, scalar2=1e-6, op0=ALU.mult, op1=ALU.add),
                         r=[('ssq', t)], w=[('ms', t)])
                    S.op('gpsimd', lambda e: e.tensor_tensor(out=stat[:, 48 + t:49 + t], in0=stat[:, 24 + t:25 + t], in1=der[:, Q_NEGH:Q_NEGH + 1], op=ALU.pow),
                         r=[('ms', t), 'negh'], w=[('rstd', t)])

                gcount = [0]

                def stC(t):
                    b4 = t % 4
                    b3 = t % 2
                    gi = t // 2
                    tt = t % 2
                    S.op('vector', lambda e: e.tensor_scalar(out=xs[:, b3 * D:(b3 + 1) * D], in0=xin[:, b4 * D:(b4 + 1) * D], scalar1=stat[:, 48 + t:49 + t], scalar2=None, op0=ALU.mult),
                         r=[('xin', b4), ('rstd', t)], w=[('xs', b3)])

                    def tr_fn(e):
                        i = None
                        for k in range(8):
                            c0 = (k // 2) * 512 + (k % 2) * 256 + tt * 128
                            i = e.transpose(out=ps[:, c0:c0 + 128], in_=xs[:, b3 * D + k * 128:b3 * D + (k + 1) * 128], identity=ident[:])
                        return i
                    S.op('tensor', tr_fn, r=[('xs', b3), 'ident'], w=[('ps', 0), ('ps', 1), ('ps', 2), ('ps', 3)])
                    if tt == 1:
                        if gi == 0:
                            ada_stage()
                        gmo, sho = (Q_GMC, Q_MODC) if gi == 0 else (Q_GMX, Q_MODX)
                        gmn, shn = ('gmc', 'modc') if gi == 0 else ('gmx', 'modx')
                        for k in range(8):
                            c0 = (k // 2) * 512 + (k % 2) * 256
                            dst = xmT[:, k * TT + gi * 256:k * TT + (gi + 1) * 256]
                            if k < 4:
                                S.op('scalar', lambda e, c0=c0, dst=dst, k=k: e.activation(
                                    out=dst, in_=ps[:, c0:c0 + 256], func=AF.Identity, scale=der[:, gmo + k:gmo + k + 1], bias=der[:, sho + k:sho + k + 1]),
                                    r=[('ps', k // 2), gmn, shn], w=[('xmT', gi, k)])
                            else:
                                S.op('vector', lambda e, c0=c0, dst=dst, k=k: e.tensor_scalar(
                                    out=dst, in0=ps[:, c0:c0 + 256], scalar1=der[:, gmo + k:gmo + k + 1], scalar2=der[:, sho + k:sho + k + 1], op0=ALU.mult, op1=ALU.add),
                                    r=[('ps', k // 2), gmn, shn], w=[('xmT', gi, k)])
                        if gi == 8:
                            gx_stage()
                        gjobs = []
                        if gi >= 3:
                            gjobs.append(((gi - 3) // 2, range(0, 4) if gi % 2 == 1 else range(4, 8)))
                        if gi == 8:
                            gjobs.append((3, range(8)))
                        for (j, heads) in gjobs:
                            if j == 3:
                                stPa(0)
                                stPc(0, 0)
                                stPc(0, 1)
                            for h in heads:
                                bk = 4 + gcount[0] % 4
                                gcount[0] += 1

                                def g_mm(e, h=h, j=j, bk=bk):
                                    i = None
                                    for k in range(8):
                                        i = e.matmul(bank(bk), lhsT=Wgv[:, k * 1024 + h * 128:k * 1024 + (h + 1) * 128],
                                                     rhs=xmT[:, k * TT + C + j * 512:k * TT + C + (j + 1) * 512], start=(k == 0), stop=(k == 7))
                                    return i
                                S.op('tensor', g_mm, r=['Wgv'] + [('xmT', g2, k) for g2 in (2 * j + 1, 2 * j + 2) for k in range(8)], w=[('ps', bk)])
                                S.op('scalar', lambda e, h=h, j=j, bk=bk: e.activation(out=Ylru[:, h * T + j * 512:h * T + (j + 1) * 512], in_=bank(bk), func=AF.Silu),
                                     r=[('ps', bk)], w=[('Y', h, j)])

                for s_ in range(18 + 2):
                    if s_ == 13:
                        late_loads()
                    if s_ < 18:
                        stA(s_)
                    if 0 <= s_ - 1 < 18:
                        stB(s_ - 1)
                    if 0 <= s_ - 2 < 18:
                        stC(s_ - 2)
            S.barrier()

            with contextlib.ExitStack() as _es:
                xcB = _es.enter_context(nc.sbuf_tensor("xcB", [128, NB], F32))
                xcbB = _es.enter_context(nc.sbuf_tensor("xcbB", [128, NB], BF16))
                xcs[1] = xcB
                xcbs[1] = xcbB
                trb = _es.enter_context(nc.sbuf_tensor("trb", [128, 3 * NB], F32))
                tib = _es.enter_context(nc.sbuf_tensor("tib", [128, 3 * NB], F32))
                ab = _es.enter_context(nc.sbuf_tensor("ab", [128, 3 * NB], F32))
                gpc = [0]

                def stG(h, d, si):
                    hp = h % 2
                    xo = hp * NB
                    do = si * NB
                    col = d * 8 + h
                    for gate, dstb, hbo, nm in ((0, trb, Q_HBA, 'tr'), (1, tib, Q_HBX, 'ti')):
                        woff = ((gate * 2 + d) * 8 + h) * 128
                        for (c0, c1) in ((0, C), (C, C + 1024), (C + 1024, NB)):
                            pp = gpc[0] % 3
                            gpc[0] += 1
                            n = c1 - c0

                            def gate_mm(e, woff=woff, c0=c0, c1=c1, pp=pp):
                                i = None
                                for q0 in range(c0, c1, 512):
                                    q1 = min(q0 + 512, c1)
                                    i = e.matmul(ps[:, pp * 1024 + (q0 - c0):pp * 1024 + (q1 - c0)], lhsT=Wg[:, woff:woff + 128], rhs=xcbs[hp][:, q0:q1],
                                                 start=True, stop=True)
                                return i
                            S.op('tensor', gate_mm, r=[('Wg', 0), ('Wg', 1), ('xcb', hp)], w=[('ps', 2 * pp), ('ps', 2 * pp + 1)])
                            S.op('scalar', lambda e, dstb=dstb, c0=c0, c1=c1, pp=pp, n=n, hbo=hbo: e.activation(
                                out=dstb[:, do + c0:do + c1], in_=ps[:, pp * 1024:pp * 1024 + n], func=AF.Tanh, scale=0.5, bias=der[:, hbo + col:hbo + col + 1]),
                                r=[('ps', 2 * pp), ('ps', 2 * pp + 1), 'hba', 'hbx'], w=[(nm, si)])
                    S.op('scalar', lambda e: e.activation(out=ab[:, do:do + NB], in_=trb[:, do:do + NB], func=AF.Exp,
                                                          scale=der[:, Q_CH + col:Q_CH + col + 1], bias=der[:, Q_CH + col:Q_CH + col + 1]),
                         r=[('tr', si), 'ch'], w=[('a', si)])
                    S.op('scalar', lambda e: e.activation(out=trb[:, do:do + NB], in_=trb[:, do:do + NB], func=AF.Exp,
                                                          scale=der[:, Q_CL + col:Q_CL + col + 1], bias=der[:, Q_CL + col:Q_CL + col + 1]),
                         r=[('tr', si), 'cl'], w=[('tr', si)])
                    S.op('scalar', lambda e: e.activation(out=trb[:, do:do + NB], in_=trb[:, do:do + NB], func=AF.Sqrt, scale=-0.25, bias=0.25),
                         r=[('tr', si)], w=[('tr', si)])

                def stD1(h, d, si):
                    hp = h % 2
                    xo = hp * NB
                    do = si * NB
                    S.op('vector', lambda e: e.scalar_tensor_tensor(out=tib[:, do:do + NB], in0=tib[:, do:do + NB], scalar=1.0, in1=xcs[hp][:], op0=ALU.add, op1=ALU.mult),
                         r=[('ti', si), ('xc', hp, 0), ('xc', hp, C)], w=[('ti', si)])
                    S.op('vector', lambda e: e.tensor_tensor(out=tib[:, do:do + NB], in0=tib[:, do:do + NB], in1=trb[:, do:do + NB], op=ALU.mult),
                         r=[('ti', si), ('tr', si)], w=[('ti', si)])

                def stD2(h, d, si, s0):
                    do = si * NB
                    if d == 0:
                        S.op('vector', lambda e: e.tensor_tensor_scan(out=trb[:, do:do + NB], data0=ab[:, do:do + NB], data1=tib[:, do:do + NB], initial=0.0, op0=ALU.mult, op1=ALU.add),
                             r=[('a', si), ('ti', si)], w=[('tr', si)])
                    else:
                        S.op('vector', lambda e: e.tensor_tensor_scan(out=trb[:, do:do + C][:, ::-1], data0=ab[:, do:do + C][:, ::-1], data1=tib[:, do:do + C][:, ::-1],
                                                                      initial=0.0, op0=ALU.mult, op1=ALU.add), r=[('a', si), ('ti', si)], w=[('tr', si)])
                        S.op('vector', lambda e: e.tensor_tensor_scan(out=trb[:, do + C:do + NB][:, ::-1], data0=ab[:, do + C:do + NB][:, ::-1], data1=tib[:, do + C:do + NB][:, ::-1],
                                                                      initial=trb[:, do:do + 1], op0=ALU.mult, op1=ALU.add), r=[('a', si), ('ti', si), ('tr', si)], w=[('tr', si)])
                        d0 = s0 * NB
                        S.op('vector', lambda e: e.tensor_tensor(out=ab[:, do + C:do + NB], in0=trb[:, d0 + C:d0 + NB], in1=trb[:, do + C:do + NB], op=ALU.add),
                             r=[('tr', s0), ('tr', si), ('a', si)], w=[('a', si)])
                        S.op('vector', lambda e: e.tensor_tensor(out=Ylru[:, h * T:(h + 1) * T], in0=ab[:, do + C:do + NB], in1=Ylru[:, h * T:(h + 1) * T], op=ALU.mult),
                             r=[('a', si)], w=[('Yf', h)])

                units = [(h, d) for h in range(8) for d in range(2)]
                for s_ in range(17):
                    if s_ >= 1:
                        ph, pd = units[s_ - 1]
                        if pd == 1 and ph + 2 < 8:
                            stPa(ph + 2)
                    if s_ < 16:
                        stG(units[s_][0], units[s_][1], s_ % 3)
                    if s_ == 0:
                        stPa(1)
                        stPc(1, 0)
                        stPc(1, 1)
                    if s_ >= 1:
                        stD1(ph, pd, (s_ - 1) % 3)
                        stD2(ph, pd, (s_ - 1) % 3, (s_ - 2) % 3)
                        if pd == 1 and ph + 2 < 8:
                            stPc(ph + 2, 0)
                        if pd == 0 and 1 <= ph and ph + 1 < 8:
                            stPc(ph + 1, 1)
                    if s_ == 15:
                        S.op('gpsimd', lambda e: e.dma_start(out=k3(late['Wv'][:], 1024), in_=winv[:, :, 3072:4096]),
                             w=[('Wg', 0), ('Wg', 1), ('Wxs', 0), ('Wxs', 1), 'xa_c', 'xa_x'], dma=('wc', 'Wv'))
                        S.op('gpsimd', lambda e: e.dma_start(out=k3(late['Wu'][:], 1024), in_=winv[:, :, 2048:3072]),
                             w=['xa_x', ('xc', 0, 0), ('xc', 0, C), ('xcb', 0)], dma=('wc', 'Wu'))
            S.barrier()

        with contextlib.ExitStack() as _es:
            Wv = _es.enter_context(nc.sbuf_tensor("Wv", [128, 8 * 1024], BF16))
            late['Wv'] = Wv
            Wu = _es.enter_context(nc.sbuf_tensor("Wu", [128, 8 * 1024], BF16))
            late['Wu'] = Wu
            Wgb = _es.enter_context(nc.sbuf_tensor("Wgb", [128, 8 * 1024], BF16))
            wob = _es.enter_context(nc.sbuf_tensor("wob", [128, 16 * 1024], BF16))
            fgb = _es.enter_context(nc.sbuf_tensor("fgb", [128, D], F32))
            Bt = _es.enter_context(nc.sbuf_tensor("Bt", [128, 8 * 128], F32))
            wsTb = _es.enter_context(nc.sbuf_tensor("wsTb", [128, 1024], BF16))
            gv = _es.enter_context(nc.sbuf_tensor("gv", [128, D], F32))
            vhat = _es.enter_context(nc.sbuf_tensor("vhat", [128, 2 * D], BF16))
            gu = _es.enter_context(nc.sbuf_tensor("gu", [128, 8 * 256], F32))
            sgb = _es.enter_context(nc.sbuf_tensor("sgb", [128, 8 * 256], F32))
            Ysgu = _es.enter_context(nc.sbuf_tensor("Ysgu", [128, 8 * 256], BF16))
            xres = _es.enter_context(nc.sbuf_tensor("xres", [128, D], F32))
            xn = _es.enter_context(nc.sbuf_tensor("xn", [128, 2 * D], F32))
            st3 = _es.enter_context(nc.sbuf_tensor("st3", [128, 96], F32))
            S.op('gpsimd', lambda e: e.dma_start(out=wsTb[:], in_=wsT_d[:, :]), w=['wsTb'], dma=('wc', 'wsTb'))
            S.op('gpsimd', lambda e: e.dma_start(out=k3(Wgb[:], 1024), in_=winv[:, :, 4096:5120]), w=['Wgb'], dma=('wc', 'Wgb'))
            for hh in range(2):
                S.op('gpsimd', lambda e, hh=hh: e.dma_start(out=k3(wob[:, hh * 8192:(hh + 1) * 8192], 1024), in_=woutv[:, hh * 8:(hh + 1) * 8, :]),
                     w=[('wob', hh)], dma=('wc', 'wob', hh))
            ld(fgb[:], fgrep_d[:, :], ['fgb'])
            ld(xn[:, 0:D], sbrep_d[:, :], [('xn', 0)])

            def st_bt():
                def rs_mm(e):
                    i = None
                    for hf in range(2):
                        i = e.matmul(bank(6 + hf), lhsT=ones_b[:], rhs=wsTb[:, hf * 512:(hf + 1) * 512], start=True, stop=True)
                    return i
                S.op('tensor', rs_mm, r=['ones_b', 'wsTb'], w=[('ps', 6), ('ps', 7)])
                for g in range(8):
                    S.op('vector', lambda e, g=g: e.scalar_tensor_tensor(
                        out=Bt[:, g * 128:(g + 1) * 128], in0=ps[:, 6 * 512 + g * 128:6 * 512 + (g + 1) * 128],
                        scalar=par[:, P_LNB + g:P_LNB + g + 1], in1=xn[:, g * 128:(g + 1) * 128], op0=ALU.mult, op1=ALU.add),
                        r=[('ps', 6), ('ps', 7), ('xn', 0)], w=[('Bt', g)])

            ci = [0]

            def st_v(j, n):
                X0 = C + j * 256
                if True:
                    ci[0] += 1
                    so = (ci[0] % 4) * 16

                    def v_mm(e, n=n):
                        i = None
                        for hf in range(2):
                            for k in range(8):
                                i = e.matmul(bank(hf), lhsT=xmT[:, k * TT + X0 + n * 128:k * TT + X0 + (n + 1) * 128],
                                             rhs=Wv[:, k * 1024 + hf * 512:k * 1024 + (hf + 1) * 512], start=(k == 0), stop=(k == 7))
                        return i
                    S.op('tensor', v_mm, r=['Wv'], w=[('ps', 0), ('ps', 1)])
                    S.op('scalar', lambda e: e.activation(out=gv[:], in_=ps[:, 0:1024], func=AF.Gelu_apprx_tanh), r=[('ps', 0), ('ps', 1)], w=['gv'])
                    for hf in range(2):
                        S.op('vector', lambda e, hf=hf, so=so: e.bn_stats(out=st3[:, so + hf * 6:so + hf * 6 + 6], in_=gv[:, hf * 512:(hf + 1) * 512]),
                             r=['gv'], w=[('bst', so, hf)])
                    S.op('vector', lambda e, so=so: e.bn_aggr(out=st3[:, so + 12:so + 14], in_=st3[:, so:so + 12]), r=[('bst', so, 0), ('bst', so, 1)], w=[('mv', so)])
                    S.op('vector', lambda e, so=so: e.tensor_scalar(out=st3[:, so + 14:so + 15], in0=st3[:, so + 13:so + 14], scalar1=1e-5, scalar2=None, op0=ALU.add),
                         r=[('mv', so)], w=[('ve', so)])
                    S.op('gpsimd', lambda e, so=so: e.tensor_tensor(out=st3[:, so + 15:so + 16], in0=st3[:, so + 14:so + 15], in1=der[:, Q_NEGH:Q_NEGH + 1], op=ALU.pow),
                         r=[('ve', so)], w=[('rs', so)])
                    S.op('vector', lambda e, so=so, n=n: e.tensor_scalar(out=vhat[:, n * D:(n + 1) * D], in0=gv[:], scalar1=st3[:, so + 12:so + 13],
                                                                          scalar2=st3[:, so + 15:so + 16], op0=ALU.subtract, op1=ALU.mult),
                         r=['gv', ('mv', so), ('rs', so)], w=[('vhat', n)])

            def st_ug(j, which, pairs):
                X0 = C + j * 256
                for (Wt, wn, dst, dn, fn_) in (((Wu, 'Wu', gu, 'gu', AF.Gelu_apprx_tanh), (Wgb, 'Wgb', sgb, 'sgb', AF.Silu))[which],):
                    for gpair in pairs:
                        bk = 4 + gpair % 2

                        def ug_mm(e, gpair=gpair, bk=bk, Wt=Wt):
                            i = None
                            for g in (2 * gpair, 2 * gpair + 1):
                                co = (g % 2) * 256
                                for k in range(8):
                                    i = e.matmul(bank(bk, co, co + 256), lhsT=Wt[:, k * 1024 + g * 128:k * 1024 + (g + 1) * 128], rhs=xmT[:, k * TT + X0:k * TT + X0 + 256],
                                                 start=(k == 0), stop=(k == 7), skip_group_check=True)
                            return i
                        S.op('tensor', ug_mm, r=[wn], w=[('ps', bk)])
                        S.op('scalar', lambda e, gpair=gpair, bk=bk, dst=dst, fn_=fn_: e.activation(out=dst[:, gpair * 512:(gpair + 1) * 512], in_=bank(bk), func=fn_),
                             r=[('ps', bk)], w=[(dn, gpair)])

            def st_gg(j):
                S.op('vector', lambda e: e.tensor_tensor(out=gu[:], in0=gu[:], in1=sgb[:], op=ALU.mult),
                     r=[('gu', g) for g in range(4)] + [('sgb', g) for g in range(4)], w=[('gu', g) for g in range(4)])

            def st_mix(j, pairs, final):
                for gpair in pairs:
                    bk = 6 + gpair % 2

                    def mix_mm(e, gpair=gpair, bk=bk):
                        i = None
                        for g in (2 * gpair, 2 * gpair + 1):
                            co = (g % 2) * 256
                            for n in range(2):
                                i = e.matmul(bank(bk, co + n * 128, co + (n + 1) * 128), lhsT=vhat[:, n * D + g * 128:n * D + (g + 1) * 128], rhs=wsTb[:, g * 128:(g + 1) * 128],
                                             start=True, stop=True, skip_group_check=True)
                        return i
                    S.op('tensor', mix_mm, r=[('vhat', 0), ('vhat', 1), 'wsTb'], w=[('ps', bk)])
                    for g in (2 * gpair, 2 * gpair + 1):
                        co = (g % 2) * 256
                        for n in range(2):
                            S.op('vector', lambda e, g=g, bk=bk, co=co, n=n: e.scalar_tensor_tensor(
                                out=sgb[:, g * 256 + n * 128:g * 256 + (n + 1) * 128], in0=bank(bk, co + n * 128, co + (n + 1) * 128),
                                scalar=par[:, P_LNG + g:P_LNG + g + 1], in1=Bt[:, g * 128:(g + 1) * 128], op0=ALU.mult, op1=ALU.add),
                                r=[('ps', bk), 'p_lng', ('Bt', g)], w=[('sgb', gpair)])
                if final:
                    S.op('vector', lambda e: e.tensor_tensor(out=Ysgu[:], in0=sgb[:], in1=gu[:], op=ALU.mult),
                         r=[('gu', g) for g in range(4)] + [('sgb', g) for g in range(4)], w=['Ysgu'])

            def st_o1(j, n):
                tg = j * 2 + n
                ob = tg % 2
                so = 64 + (tg % 4) * 4
                pb = 2 if n == 0 else 0
                S.op('sync', lambda e: e.dma_start(out=xres[:], in_=x_d[tg * 128:(tg + 1) * 128, :]), w=['xres'], dma=('xr', 0))

                def o_mm(e):
                    i = None
                    for hf in range(2):
                        for kc in range(16):
                            if kc < 8:
                                lt = Ylru[:, kc * T + j * 256 + n * 128:kc * T + j * 256 + (n + 1) * 128]
                            else:
                                lt = Ysgu[:, (kc - 8) * 256 + n * 128:(kc - 8) * 256 + (n + 1) * 128]
                            i = e.matmul(bank(pb + hf), lhsT=lt, rhs=wob[:, kc * 1024 + hf * 512:kc * 1024 + (hf + 1) * 512], start=(kc == 0), stop=(kc == 15))
                    return i
                S.op('tensor', o_mm, r=['Ysgu', ('wob', 0), ('wob', 1)], w=[('ps', pb), ('ps', pb + 1)])
                S.op('vector', lambda e: e.tensor_tensor(out=xn[:, ob * D:(ob + 1) * D], in0=ps[:, pb * 512:(pb + 2) * 512], in1=gxb[:], op=ALU.mult),
                     r=[('ps', pb), ('ps', pb + 1), 'gxb'] + [('Bt', g) for g in range(8)], w=[('xn', ob)])
                S.op('vector', lambda e: e.tensor_tensor(out=xn[:, ob * D:(ob + 1) * D], in0=xn[:, ob * D:(ob + 1) * D], in1=xres[:], op=ALU.add),
                     r=[('xn', ob), 'xres'], w=[('xn', ob)])
                S.op('scalar', lambda e: e.activation(out=xres[:], in_=xn[:, ob * D:(ob + 1) * D], func=AF.Square, accum_out=st3[:, so:so + 1]),
                     r=[('xn', ob)], w=['xres', ('q0', so)])

            def st_o2(j, n):
                tg = j * 2 + n
                so = 64 + (tg % 4) * 4
                S.op('vector', lambda e: e.tensor_scalar(out=st3[:, so + 1:so + 2], in0=st3[:, so:so + 1], scalar1=1.0 / D, scalar2=1e-6, op0=ALU.mult, op1=ALU.add),
                     r=[('q0', so)], w=[('q1', so)])
                S.op('gpsimd', lambda e: e.tensor_tensor(out=st3[:, so + 2:so + 3], in0=st3[:, so + 1:so + 2], in1=der[:, Q_NEGH:Q_NEGH + 1], op=ALU.pow),
                     r=[('q1', so)], w=[('q2', so)])

            def st_o3(j, n):
                tg = j * 2 + n
                ob = tg % 2
                so = 64 + (tg % 4) * 4
                S.op('vector', lambda e: e.scalar_tensor_tensor(out=xn[:, ob * D:(ob + 1) * D], in0=xn[:, ob * D:(ob + 1) * D], scalar=st3[:, so + 2:so + 3],
                                                                in1=fgb[:], op0=ALU.mult, op1=ALU.mult),
                     r=[('xn', ob), ('q2', so), 'fgb'], w=[('xn', ob)])
                S.op('sync', lambda e: e.dma_start(out=out_d[tg * 128:(tg + 1) * 128, :], in_=xn[:, ob * D:(ob + 1) * D]), r=[('xn', ob)], dma=('st', ob))

            for j in range(8):
                st_v(j, 0)
                st_ug(j, 0, (0, 1))
                st_ug(j, 0, (2, 3))
                st_v(j, 1)
                st_ug(j, 1, (0, 1, 2, 3))
                st_gg(j)
                if j == 0:
                    st_bt()
                if j >= 1:
                    st_o1(j - 1, 0)
                st_mix(j, (0, 1), False)
                if j >= 1:
                    st_o1(j - 1, 1)
                st_mix(j, (2, 3), True)
                if j >= 1:
                    st_o2(j - 1, 0)
                    st_o2(j - 1, 1)
                    st_o3(j - 1, 0)
                    st_o3(j - 1, 1)
            for n in range(2):
                st_o1(7, n)
            for n in range(2):
                st_o2(7, n)
            for n in range(2):
                st_o3(7, n)

        keys = list(S.cnt.keys())
        with contextlib.ExitStack() as es:
            sems = {}
            for i, k in enumerate(keys):
                sems[k] = es.enter_context(nc.semaphore("s%d" % i))
            block = es.enter_context(nc.Block())
            S.emit(nc, block, sems)
    return nc


_NC_CACHE = {}


def kernel(x, c, ctx, c_ctx, ada_w, ada_b, norm_g, w_in, conv_w, conv_b, lru_wa, lru_ba, lru_wx, lru_bx,
           lru_lambda, sgu_ln_g, sgu_ln_b, sgu_w, sgu_b, w_out, final_g):
    f = lambda a: np.ascontiguousarray(np.asarray(a, dtype=np.float32))
    x = f(x); c = f(c); ctx = f(ctx); c_ctx = f(c_ctx)
    ada_w0 = f(ada_w[0]); ada_b0 = f(ada_b[0]); ng = f(norm_g[0]); w_in0 = f(w_in[0]); w_out0 = f(w_out[0])

    def colT(v, nchunk):
        return f(np.asarray(v, dtype=np.float32).reshape(nchunk, 128).T)
    adabT = colT(ada_b0, 24)
    bg_rep = f(np.broadcast_to(ada_b0[2048:3072][None, :], (128, 1024)))
    fg_rep = f(np.broadcast_to(np.asarray(final_g, dtype=np.float32)[None, :], (128, 1024)))
    ngT = colT(ng, 8)
    cwT = f(np.asarray(conv_w[0], dtype=np.float32).reshape(4, 8, 128).transpose(2, 1, 0).reshape(128, 32))
    cbT = colT(conv_b[0], 8)
    wa = np.asarray(lru_wa[0], dtype=np.float32)
    wx = np.asarray(lru_wx[0], dtype=np.float32)
    wg = f(np.stack([wa, wx], 0).transpose(3, 0, 1, 2, 4).reshape(128, 4096))
    baT = f(np.asarray(lru_ba[0], dtype=np.float32).transpose(2, 0, 1).reshape(128, 16))
    bxT = f(np.asarray(lru_bx[0], dtype=np.float32).transpose(2, 0, 1).reshape(128, 16))
    lamT = f(np.asarray(lru_lambda[0], dtype=np.float32).reshape(2, 8, 128).transpose(2, 0, 1).reshape(128, 16))
    lngT = colT(sgu_ln_g[0], 8)
    lnbT = colT(sgu_ln_b[0], 8)
    wsT = f(np.asarray(sgu_w[0], dtype=np.float32).transpose(2, 0, 1).reshape(128, 1024))
    sb_rep = f(np.broadcast_to(np.asarray(sgu_b[0], dtype=np.float32).reshape(1, 1024), (128, 1024)))

    if 'nc' not in _NC_CACHE:
        _NC_CACHE['nc'] = build_nc()
    nc = _NC_CACHE['nc']
    in_maps = []
    for b in range(NCORES):
        cT = f(np.stack([c[b].reshape(8, 128).T, c_ctx.reshape(8, 128).T], axis=2).reshape(128, 16))
        in_maps.append({
            "x": x[b], "ctx": ctx[b], "cT": cT, "ada_w": ada_w0, "adabT": adabT, "bg_rep": bg_rep, "fg_rep": fg_rep,
            "ngT": ngT, "w_in": w_in0, "w_out": w_out0, "cwT": cwT, "cbT": cbT, "wg": wg, "baT": baT, "bxT": bxT,
            "lamT": lamT, "lngT": lngT, "lnbT": lnbT, "wsT": wsT, "sb_rep": sb_rep,
        })
    res = run_bass_kernel_spmd(nc, in_maps, core_ids=list(range(NCORES)))
    return np.stack([np.asarray(r["out"], dtype=np.float32) for r in res.results], axis=0)
```

```python
import contextlib
import numpy as np
import concourse.bass as bass
import concourse.mybir as mybir
from concourse.bass_utils import run_bass_kernel_spmd
from concourse.alu_op_type import AluOpType as ALU

F32 = mybir.dt.float32
BF16 = mybir.dt.bfloat16
AF = mybir.ActivationFunctionType

D = 1024
T = 2048
C = 256
TT = T + C
NCORES = 8
ENGS = ['sync', 'scalar', 'vector', 'gpsimd', 'tensor']


class Sched:
    def __init__(self):
        self.all = []
        self.cnt = {}
        self.lastw = {}
        self.readers = {}

    def op(self, eng, fn, r=(), w=(), dma=None):
        key = ('d', dma) if dma else ('c', eng)
        inc = 16 if dma else 1
        self.cnt[key] = self.cnt.get(key, 0) + inc
        tok = (key, self.cnt[key])
        deps = {}

        def add(k, v):
            if deps.get(k, 0) < v:
                deps[k] = v
        xs_ = [s for s in r if isinstance(s, tuple) and s[0] == 'ps']
        r = [s for s in r if not (isinstance(s, tuple) and s[0] == 'ps')]
        for s in r:
            t = self.lastw.get(s)
            if t: add(t[0], t[1])
        for s in w:
            t = self.lastw.get(s)
            if t: add(t[0], t[1])
            for k, v in self.readers.get(s, {}).items():
                add(k, v)
        for s in xs_:
            t = self.lastw.get(s)
            if t and not (t[0] == key and t[2]):
                add(t[0], t[1])
            for k, v in self.readers.get(s, {}).items():
                add(k, v)
        for s in r:
            rd = self.readers.setdefault(s, {})
            if rd.get(key, 0) < tok[1]:
                rd[key] = tok[1]
        for s in w:
            self.lastw[s] = (tok[0], tok[1], False)
            self.readers[s] = {}
        for s in xs_:
            self.lastw[s] = (tok[0], tok[1], True)
            self.readers[s] = {}
        self.all.append((eng, fn, deps, key, inc))
        return tok

    def barrier(self):
        allc = dict(self.cnt)
        for e in ENGS:
            self.all.append((e, None, dict(allc), None, 0))
        self.lastw = {}
        self.readers = {}

    def emit(self, nc, block, sems, limit=None):
        ops = self.all if limit is None else self.all[:limit]
        final = {}
        for (eng, fn, deps, key, inc) in ops:
            if fn is not None:
                final[key] = final.get(key, 0) + inc
        for eng in ENGS:
            q = [o for o in ops if o[0] == eng]

            def body(e, q=q, eng=eng):
                waited = {}
                for (_, fn, deps, key, inc) in q:
                    for k, v in deps.items():
                        if eng == 'tensor' and k == ('c', 'tensor'):
                            continue
                        if waited.get(k, 0) < v:
                            e.wait_ge(sems[k], v)
                            waited[k] = v
                    if fn is not None:
                        ins = fn(e)
                        ins.then_inc(sems[key], inc)
                for k, v in final.items():
                    if waited.get(k, 0) < v:
                        e.wait_ge(sems[k], v)
            getattr(block, eng)(body)


def build_nc():
    nc = bass.Bass("TRN2", target_bir_lowering=False)

    def din(name, shape):
        return nc.dram_tensor(name, list(shape), F32, kind="ExternalInput").ap()
    x_d = din("x", [T, D])
    ctx_d = din("ctx", [C, D])
    cT_d = din("cT", [128, 16])
    adaw_d = din("ada_w", [D, 3 * D])
    adabT_d = din("adabT", [128, 24])
    bgrep_d = din("bg_rep", [128, D])
    fgrep_d = din("fg_rep", [128, D])
    ngT_d = din("ngT", [128, 8])
    win_d = din("w_in", [D, 5 * D])
    wout_d = din("w_out", [2 * D, D])
    cwT_d = din("cwT", [128, 32])
    cbT_d = din("cbT", [128, 8])
    wg_d = din("wg", [128, 4096])
    baT_d = din("baT", [128, 16])
    bxT_d = din("bxT", [128, 16])
    lamT_d = din("lamT", [128, 16])
    lngT_d = din("lngT", [128, 8])
    lnbT_d = din("lnbT", [128, 8])
    wsT_d = din("wsT", [128, 1024])
    sbrep_d = din("sb_rep", [128, 1024])
    out_d = nc.dram_tensor("out", [T, D], F32, kind="ExternalOutput").ap()

    S = Sched()
    late = {}
    winv = win_d.rearrange("(k p) n -> p k n", p=128)
    woutv = wout_d.rearrange("(k p) n -> p k n", p=128)
    adawv = adaw_d.rearrange("(k p) n -> p k n", p=128)

    P_CT, P_ADAB, P_NG, P_CW, P_CB, P_BA, P_BX, P_LAM, P_LNG, P_LNB = 0, 16, 40, 48, 80, 88, 104, 120, 136, 144
    NPAR = 152
    Q_CS, Q_MODX, Q_MODC, Q_GMX, Q_GMC, Q_E1, Q_SP, Q_CL, Q_CH, Q_HBA, Q_HBX, Q_NEGH = 0, 16, 32, 48, 56, 64, 80, 96, 112, 128, 144, 160
    NDER = 168
    NB = TT

    def k3(t, n):
        return t.rearrange("p (k n) -> p k n", n=n)

    with contextlib.ExitStack() as _es:
        par = _es.enter_context(nc.sbuf_tensor("par", [128, NPAR], F32))
        der = _es.enter_context(nc.sbuf_tensor("der", [128, NDER], F32))
        stat = _es.enter_context(nc.sbuf_tensor("stat", [128, 4 * 24], F32))
        ident = _es.enter_context(nc.sbuf_tensor("ident", [128, 128], F32))
        ones_b = _es.enter_context(nc.sbuf_tensor("ones_b", [128, 128], BF16))
        csbf = _es.enter_context(nc.sbuf_tensor("csbf", [128, 16], BF16))
        gxb = _es.enter_context(nc.sbuf_tensor("gxb", [128, D], F32))
        xmT = _es.enter_context(nc.sbuf_tensor("xmT", [128, 8 * TT], BF16))
        Ylru = _es.enter_context(nc.sbuf_tensor("Ylru", [128, 8 * T], BF16))
        ps = _es.enter_context(nc.psum_tensor("ps", [128, 4096], F32))

        def bank(i, c0=0, c1=512):
            return ps[:, i * 512 + c0:i * 512 + c1]

        def ld(dst, src, w, r=()):
            S.op('sync', lambda e: e.dma_start(out=dst, in_=src), r=r, w=w, dma=('par', w[0]))

        with contextlib.ExitStack() as _es:
            Wg = _es.enter_context(nc.sbuf_tensor("Wg", [128, 4096], BF16))
            Wxs = _es.enter_context(nc.sbuf_tensor("Wxs", [128, 2 * 1024], BF16))
            xa_c = _es.enter_context(nc.sbuf_tensor("xa_c", [128, C + 3], F32))
            xa_x = _es.enter_context(nc.sbuf_tensor("xa_x", [128, T + 3], F32))
            xcA = _es.enter_context(nc.sbuf_tensor("xcA", [128, NB], F32))
            xcbA = _es.enter_context(nc.sbuf_tensor("xcbA", [128, NB], BF16))
            ibc = [0]
            def wxa_load(h):
                S.op('gpsimd', lambda e, h=h: e.dma_start(out=k3(Wxs[:, (h % 2) * 1024:(h % 2 + 1) * 1024], 128), in_=winv[:, :, h * 128:(h + 1) * 128]),
                     w=[('Wxs', h % 2)], dma=('wc', 'Wxs', h % 2))

            xcs = [xcA, None]
            xcbs = [xcbA, None]

            def stPa(h):
                hp = h % 2
                for j in range(5):
                    c0, c1 = (0, C) if j == 0 else (C + (j - 1) * 512, C + j * 512)
                    n = c1 - c0
                    bk = 6 + (ibc[0] % 2)
                    ibc[0] += 1

                    def xa_mm(e, c0=c0, c1=c1, n=n, bk=bk):
                        i = None
                        for k in range(8):
                            i = e.matmul(bank(bk, 0, n), lhsT=Wxs[:, hp * 1024 + k * 128:hp * 1024 + (k + 1) * 128],
                                         rhs=xmT[:, k * TT + c0:k * TT + c1], start=(k == 0), stop=(k == 7))
                        return i
                    S.op('tensor', xa_mm, r=[('Wxs', hp)] + [('xmT', g2, k) for g2 in range(c0 // 256, (c1 + 255) // 256) for k in range(8)], w=[('ps', bk)])
                    if j == 0:
                        S.op('scalar', lambda e, bk=bk: e.activation(out=xa_c[:, 1:1 + C], in_=bank(bk, 0, C), func=AF.Identity), r=[('ps', bk)], w=['xa_c'])
                    else:
                        S.op('scalar', lambda e, bk=bk, j=j: e.activation(out=xa_x[:, 1 + (j - 1) * 512:1 + j * 512], in_=bank(bk), func=AF.Identity), r=[('ps', bk)], w=['xa_x'])
                if h + 2 < 8:
                    wxa_load(h + 2)

            def stPc(h, part):
                hp = h % 2
                xo = hp * NB
                (src, sl, L, o0) = ((xa_c, 'xa_c', C, 0), (xa_x, 'xa_x', T, C))[part]
                S.op('vector', lambda e: e.tensor_scalar(
                    out=xcs[hp][:, o0:o0 + L], in0=src[:, 0:L], scalar1=par[:, P_CW + h * 4:P_CW + h * 4 + 1], scalar2=par[:, P_CB + h:P_CB + h + 1],
                    op0=ALU.mult, op1=ALU.add), r=[sl, 'p_cw', 'p_cb'], w=[('xc', hp, o0)])
                for tp in range(1, 4):
                    S.op('vector', lambda e, tp=tp: e.scalar_tensor_tensor(
                        out=xcs[hp][:, o0:o0 + L], in0=src[:, tp:tp + L], scalar=par[:, P_CW + h * 4 + tp:P_CW + h * 4 + tp + 1], in1=xcs[hp][:, o0:o0 + L],
                        op0=ALU.mult, op1=ALU.add), r=[sl, 'p_cw', ('xc', hp, o0)], w=[('xc', hp, o0)])
                if part == 1:
                    S.op('vector', lambda e: e.tensor_copy(out=xcbs[hp][:], in_=xcs[hp][:]), r=[('xc', hp, 0), ('xc', hp, C)], w=[('xcb', hp)])


            with contextlib.ExitStack() as _es:
                adaW = _es.enter_context(nc.sbuf_tensor("adaW", [128, 8 * 3072], BF16))
                Wgv = _es.enter_context(nc.sbuf_tensor("Wga", [128, 8 * 1024], BF16))
                xin = _es.enter_context(nc.sbuf_tensor("xin", [128, 4 * D], F32))
                xs = _es.enter_context(nc.sbuf_tensor("xs", [128, 2 * D], F32))
                junk = _es.enter_context(nc.sbuf_tensor("junk", [128, D], BF16))
                csb = _es.enter_context(nc.sbuf_tensor("csb", [128, 8 * 128], BF16))
                bgrep = _es.enter_context(nc.sbuf_tensor("bgrep", [128, 1024], F32))
                ld(par[:, P_CT:P_CT + 16], cT_d[:, :], ['p_ct'])
                ld(par[:, P_ADAB:P_ADAB + 24], adabT_d[:, :], ['p_adab'])
                ld(par[:, P_NG:P_NG + 8], ngT_d[:, :], ['p_ng'])
                for sl in range(2):
                    S.op('gpsimd', lambda e, sl=sl: e.dma_start(out=k3(adaW[:], 3072)[:, :, sl * 1024:(sl + 1) * 1024], in_=adawv[:, :, sl * 1024:(sl + 1) * 1024]),
                         w=[('adaW', sl)], dma=('wc', 'adaW', sl))
                S.op('gpsimd', lambda e: e.dma_start(out=k3(Wgv[:], 1024), in_=winv[:, :, 1024:2048]), w=['Wgv'], dma=('wc', 'Wgv'))

                def late_loads(part):
                    if part == 0:
                        S.op('gpsimd', lambda e: e.dma_start(out=k3(adaW[:], 3072)[:, :, 2048:3072], in_=adawv[:, :, 2048:3072]), w=[('adaW', 2)], dma=('wc', 'adaW', 2))
                        ld(bgrep[:], bgrep_d[:, :], ['bgrep'])
                    elif part == 1:
                        for hh in range(2):
                            S.op('gpsimd', lambda e, hh=hh: e.dma_start(out=Wg[:, hh * 2048:(hh + 1) * 2048], in_=wg_d[:, hh * 2048:(hh + 1) * 2048]),
                                 w=[('Wg', hh)], dma=('wc', 'Wg', hh))
                    else:
                        wxa_load(0)
                        wxa_load(1)
                ld(par[:, P_LAM:P_LAM + 16], lamT_d[:, :], ['p_lam'])
                ld(par[:, P_BA:P_BA + 16], baT_d[:, :], ['p_ba'])
                ld(par[:, P_BX:P_BX + 16], bxT_d[:, :], ['p_bx'])
                ld(par[:, P_CW:P_CW + 32], cwT_d[:, :], ['p_cw'])
                ld(par[:, P_CB:P_CB + 8], cbT_d[:, :], ['p_cb'])
                ld(par[:, P_LNG:P_LNG + 8], lngT_d[:, :], ['p_lng'])
                ld(par[:, P_LNB:P_LNB + 8], lnbT_d[:, :], ['p_lnb'])

                S.op('gpsimd', lambda e: e.memset(ident[:], 0.0), w=['ident'])
                S.op('gpsimd', lambda e: e.affine_select(out=ident[:], in_=ident[:], pattern=[[-1, 128]], compare_op=ALU.not_equal,
                                                         fill=1.0, base=0, channel_multiplier=1), r=['ident'], w=['ident'])
                S.op('gpsimd', lambda e: e.memset(ones_b[:], 1.0), w=['ones_b'])
                S.op('gpsimd', lambda e: e.memset(xa_c[:], 0.0), w=['xa_c'])
                S.op('gpsimd', lambda e: e.memset(xa_x[:], 0.0), w=['xa_x'])
                S.op('gpsimd', lambda e: e.memset(der[:, Q_NEGH:Q_NEGH + 1], -0.5), w=['negh'])

                S.op('scalar', lambda e: e.activation(out=csbf[:], in_=par[:, P_CT:P_CT + 16], func=AF.Silu), r=['p_ct'], w=['cs'])
                S.op('scalar', lambda e: e.activation(out=der[:, Q_E1:Q_E1 + 16], in_=par[:, P_LAM:P_LAM + 16], func=AF.Exp, scale=-1.0), r=['p_lam'], w=['e1'])
                S.op('scalar', lambda e: e.activation(out=der[:, Q_SP:Q_SP + 16], in_=der[:, Q_E1:Q_E1 + 16], func=AF.Ln, bias=1.0, scale=1.0), r=['e1'], w=['sp'])
                S.op('vector', lambda e: e.tensor_scalar(out=der[:, Q_CL:Q_CL + 16], in0=der[:, Q_SP:Q_SP + 16], scalar1=-8.0, scalar2=None, op0=ALU.mult), r=['sp'], w=['cl'])
                S.op('vector', lambda e: e.tensor_scalar(out=der[:, Q_CH:Q_CH + 16], in0=der[:, Q_SP:Q_SP + 16], scalar1=-4.0, scalar2=None, op0=ALU.mult), r=['sp'], w=['ch'])
                S.op('vector', lambda e: e.tensor_scalar(out=der[:, Q_HBA:Q_HBA + 16], in0=par[:, P_BA:P_BA + 16], scalar1=0.5, scalar2=None, op0=ALU.mult), r=['p_ba'], w=['hba'])
                S.op('vector', lambda e: e.tensor_scalar(out=der[:, Q_HBX:Q_HBX + 16], in0=par[:, P_BX:P_BX + 16], scalar1=0.5, scalar2=None, op0=ALU.mult), r=['p_bx'], w=['hbx'])
                for k in range(8):
                    S.op('vector', lambda e, k=k: e.tensor_scalar(out=csb[:, k * 128:(k + 1) * 128], in0=ones_b[:], scalar1=csbf[:, 2 * k:2 * k + 1],
                                                                  scalar2=None, op0=ALU.mult), r=['ones_b', 'cs'], w=[('csb', k)])

                def ada_stage():
                    def ada_mm(e):
                        i = None
                        for fc in range(16):
                            for k in range(8):
                                i = e.matmul(ps[:, 7 * 512 + fc * 2:7 * 512 + fc * 2 + 2], lhsT=adaW[:, k * 3072 + fc * 128:k * 3072 + (fc + 1) * 128],
                                             rhs=csbf[:, 2 * k:2 * k + 2], start=(k == 0), stop=(k == 7), skip_group_check=True)
                        return i
                    S.op('tensor', ada_mm, r=[('adaW', 0), ('adaW', 1), 'cs'], w=[('ps', 7)])
                    modps = ps[:, 7 * 512:7 * 512 + 32].rearrange("p (f v) -> p f v", v=2)
                    S.op('vector', lambda e: e.tensor_tensor(out=der[:, Q_MODX:Q_MODX + 16], in0=modps[:, :, 0], in1=par[:, P_ADAB:P_ADAB + 16], op=ALU.add),
                         r=[('ps', 7), 'p_adab'], w=['modx'])
                    S.op('vector', lambda e: e.tensor_tensor(out=der[:, Q_MODC:Q_MODC + 16], in0=modps[:, :, 1], in1=par[:, P_ADAB:P_ADAB + 16], op=ALU.add),
                         r=[('ps', 7), 'p_adab'], w=['modc'])
                    S.op('vector', lambda e: e.scalar_tensor_tensor(out=der[:, Q_GMX:Q_GMX + 8], in0=der[:, Q_MODX + 8:Q_MODX + 16], scalar=1.0, in1=par[:, P_NG:P_NG + 8],
                                                                    op0=ALU.add, op1=ALU.mult), r=['modx', 'p_ng'], w=['gmx'])
                    S.op('vector', lambda e: e.scalar_tensor_tensor(out=der[:, Q_GMC:Q_GMC + 8], in0=der[:, Q_MODC + 8:Q_MODC + 16], scalar=1.0, in1=par[:, P_NG:P_NG + 8],
                                                                    op0=ALU.add, op1=ALU.mult), r=['modc', 'p_ng'], w=['gmc'])

                def gx_stage():
                    def gx_mm(e):
                        i = None
                        for hf in range(2):
                            for k in range(8):
                                i = e.matmul(bank(5 + hf), lhsT=csb[:, k * 128:(k + 1) * 128], rhs=adaW[:, k * 3072 + 2048 + hf * 512:k * 3072 + 2048 + (hf + 1) * 512],
                                             start=(k == 0), stop=(k == 7))
                        return i
                    S.op('tensor', gx_mm, r=[('adaW', 2)] + [('csb', k) for k in range(8)], w=[('ps', 5), ('ps', 6)])
                    S.op('vector', lambda e: e.tensor_tensor(out=gxb[:], in0=ps[:, 5 * 512:7 * 512], in1=bgrep[:], op=ALU.add),
                         r=[('ps', 5), ('ps', 6), 'bgrep'], w=['gxb'])

                def stA(t):
                    b4 = t % 4
                    src = ctx_d[t * 128:(t + 1) * 128, :] if t < 2 else x_d[(t - 2) * 128:(t - 1) * 128, :]
                    S.op('sync', lambda e: e.dma_start(out=xin[:, b4 * D:(b4 + 1) * D], in_=src), w=[('xin', b4)], dma=('xl', b4))
                    S.op('scalar', lambda e: e.activation(out=junk[:], in_=xin[:, b4 * D:(b4 + 1) * D], func=AF.Square, accum_out=stat[:, t:t + 1]),
                         r=[('xin', b4)], w=['junk', ('ssq', t)])

                def stB(t):
                    S.op('vector', lambda e: e.tensor_scalar(out=stat[:, 24 + t:25 + t], in0=stat[:, t:t + 1], scalar1=1.0 / D, scalar2=1e-6, op0=ALU.mult, op1=ALU.add),
                         r=[('ssq', t)], w=[('ms', t)])
                    S.op('gpsimd', lambda e: e.tensor_tensor(out=stat[:, 48 + t:49 + t], in0=stat[:, 24 + t:25 + t], in1=der[:, Q_NEGH:Q_NEGH + 1], op=ALU.pow),
                         r=[('ms', t), 'negh'], w=[('rstd', t)])

                gcount = [0]

                def stC(t):
                    b4 = t % 4
                    b3 = t % 2
                    gi = t // 2
                    tt = t % 2
                    S.op('vector', lambda e: e.tensor_scalar(out=xs[:, b3 * D:(b3 + 1) * D], in0=xin[:, b4 * D:(b4 + 1) * D], scalar1=stat[:, 48 + t:49 + t], scalar2=None, op0=ALU.mult),
                         r=[('xin', b4), ('rstd', t)], w=[('xs', b3)])

                    def tr_fn(e):
                        i = None
                        for k in range(8):
                            c0 = (k // 2) * 512 + (k % 2) * 256 + tt * 128
                            i = e.transpose(out=ps[:, c0:c0 + 128], in_=xs[:, b3 * D + k * 128:b3 * D + (k + 1) * 128], identity=ident[:])
                        return i
                    S.op('tensor', tr_fn, r=[('xs', b3), 'ident'], w=[('ps', 0), ('ps', 1), ('ps', 2), ('ps', 3)])
                    if tt == 1:
                        if gi == 0:
                            ada_stage()
                        gmo, sho = (Q_GMC, Q_MODC) if gi == 0 else (Q_GMX, Q_MODX)
                        gmn, shn = ('gmc', 'modc') if gi == 0 else ('gmx', 'modx')
                        for k in range(8):
                            c0 = (k // 2) * 512 + (k % 2) * 256
                            dst = xmT[:, k * TT + gi * 256:k * TT + (gi + 1) * 256]
                            if k < 4:
                                S.op('scalar', lambda e, c0=c0, dst=dst, k=k: e.activation(
                                    out=dst, in_=ps[:, c0:c0 + 256], func=AF.Identity, scale=der[:, gmo + k:gmo + k + 1], bias=der[:, sho + k:sho + k + 1]),
                                    r=[('ps', k // 2), gmn, shn], w=[('xmT', gi, k)])
                            else:
                                S.op('vector', lambda e, c0=c0, dst=dst, k=k: e.tensor_scalar(
                                    out=dst, in0=ps[:, c0:c0 + 256], scalar1=der[:, gmo + k:gmo + k + 1], scalar2=der[:, sho + k:sho + k + 1], op0=ALU.mult, op1=ALU.add),
                                    r=[('ps', k // 2), gmn, shn], w=[('xmT', gi, k)])
                        if gi == 8:
                            gx_stage()
                        gjobs = []
                        if gi >= 3:
                            gjobs.append(((gi - 3) // 2, range(0, 4) if gi % 2 == 1 else range(4, 8)))
                        if gi == 8:
                            gjobs.append((3, range(8)))
                        for (j, heads) in gjobs:
                            if j == 3:
                                stPa(0)
                                stPc(0, 0)
                                stPc(0, 1)
                            for h in heads:
                                bk = 4 + gcount[0] % 4
                                gcount[0] += 1

                                def g_mm(e, h=h, j=j, bk=bk):
                                    i = None
                                    for k in range(8):
                                        i = e.matmul(bank(bk), lhsT=Wgv[:, k * 1024 + h * 128:k * 1024 + (h + 1) * 128],
                                                     rhs=xmT[:, k * TT + C + j * 512:k * TT + C + (j + 1) * 512], start=(k == 0), stop=(k == 7))
                                    return i
                                S.op('tensor', g_mm, r=['Wgv'] + [('xmT', g2, k) for g2 in (2 * j + 1, 2 * j + 2) for k in range(8)], w=[('ps', bk)])
                                S.op('scalar', lambda e, h=h, j=j, bk=bk: e.activation(out=Ylru[:, h * T + j * 512:h * T + (j + 1) * 512], in_=bank(bk), func=AF.Silu),
                                     r=[('ps', bk)], w=[('Y', h, j)])

                for s_ in range(18 + 2):
                    if s_ in (12, 14, 16):
                        late_loads((s_ - 12) // 2)
                    if s_ < 18:
                        stA(s_)
                    if 0 <= s_ - 1 < 18:
                        stB(s_ - 1)
                    if 0 <= s_ - 2 < 18:
                        stC(s_ - 2)
            S.barrier()

            with contextlib.ExitStack() as _es:
                xcB = _es.enter_context(nc.sbuf_tensor("xcB", [128, NB], F32))
                xcbB = _es.enter_context(nc.sbuf_tensor("xcbB", [128, NB], BF16))
                xcs[1] = xcB
                xcbs[1] = xcbB
                trb = _es.enter_context(nc.sbuf_tensor("trb", [128, 3 * NB], F32))
                tib = _es.enter_context(nc.sbuf_tensor("tib", [128, 3 * NB], F32))
                ab = _es.enter_context(nc.sbuf_tensor("ab", [128, 3 * NB], F32))
                gpc = [0]

                def stG(h, d, si):
                    hp = h % 2
                    xo = hp * NB
                    do = si * NB
                    col = d * 8 + h
                    for gate, dstb, hbo, nm in ((0, trb, Q_HBA, 'tr'), (1, tib, Q_HBX, 'ti')):
                        woff = ((gate * 2 + d) * 8 + h) * 128
                        for (c0, c1) in ((0, C), (C, C + 1024), (C + 1024, NB)):
                            pp = gpc[0] % 3
                            gpc[0] += 1
                            n = c1 - c0

                            def gate_mm(e, woff=woff, c0=c0, c1=c1, pp=pp):
                                i = None
                                for q0 in range(c0, c1, 512):
                                    q1 = min(q0 + 512, c1)
                                    i = e.matmul(ps[:, pp * 1024 + (q0 - c0):pp * 1024 + (q1 - c0)], lhsT=Wg[:, woff:woff + 128], rhs=xcbs[hp][:, q0:q1],
                                                 start=True, stop=True)
                                return i
                            S.op('tensor', gate_mm, r=[('Wg', 0), ('Wg', 1), ('xcb', hp)], w=[('ps', 2 * pp), ('ps', 2 * pp + 1)])
                            S.op('scalar', lambda e, dstb=dstb, c0=c0, c1=c1, pp=pp, n=n, hbo=hbo: e.activation(
                                out=dstb[:, do + c0:do + c1], in_=ps[:, pp * 1024:pp * 1024 + n], func=AF.Tanh, scale=0.5, bias=der[:, hbo + col:hbo + col + 1]),
                                r=[('ps', 2 * pp), ('ps', 2 * pp + 1), 'hba', 'hbx'], w=[(nm, si)])
                    S.op('scalar', lambda e: e.activation(out=ab[:, do:do + NB], in_=trb[:, do:do + NB], func=AF.Exp,
                                                          scale=der[:, Q_CH + col:Q_CH + col + 1], bias=der[:, Q_CH + col:Q_CH + col + 1]),
                         r=[('tr', si), 'ch'], w=[('a', si)])
                    S.op('scalar', lambda e: e.activation(out=trb[:, do:do + NB], in_=trb[:, do:do + NB], func=AF.Exp,
                                                          scale=der[:, Q_CL + col:Q_CL + col + 1], bias=der[:, Q_CL + col:Q_CL + col + 1]),
                         r=[('tr', si), 'cl'], w=[('tr', si)])
                    S.op('scalar', lambda e: e.activation(out=trb[:, do:do + NB], in_=trb[:, do:do + NB], func=AF.Sqrt, scale=-0.25, bias=0.25),
                         r=[('tr', si)], w=[('tr', si)])

                def stD1(h, d, si):
                    hp = h % 2
                    xo = hp * NB
                    do = si * NB
                    S.op('vector', lambda e: e.scalar_tensor_tensor(out=tib[:, do:do + NB], in0=tib[:, do:do + NB], scalar=1.0, in1=xcs[hp][:], op0=ALU.add, op1=ALU.mult),
                         r=[('ti', si), ('xc', hp, 0), ('xc', hp, C)], w=[('ti', si)])
                    S.op('vector', lambda e: e.tensor_tensor(out=tib[:, do:do + NB], in0=tib[:, do:do + NB], in1=trb[:, do:do + NB], op=ALU.mult),
                         r=[('ti', si), ('tr', si)], w=[('ti', si)])

                def stD2(h, d, si, s0):
                    do = si * NB
                    if d == 0:
                        S.op('vector', lambda e: e.tensor_tensor_scan(out=trb[:, do:do + NB], data0=ab[:, do:do + NB], data1=tib[:, do:do + NB], initial=0.0, op0=ALU.mult, op1=ALU.add),
                             r=[('a', si), ('ti', si)], w=[('tr', si)])
                    else:
                        S.op('vector', lambda e: e.tensor_tensor_scan(out=trb[:, do:do + C][:, ::-1], data0=ab[:, do:do + C][:, ::-1], data1=tib[:, do:do + C][:, ::-1],
                                                                      initial=0.0, op0=ALU.mult, op1=ALU.add), r=[('a', si), ('ti', si)], w=[('tr', si)])
                        S.op('vector', lambda e: e.tensor_tensor_scan(out=trb[:, do + C:do + NB][:, ::-1], data0=ab[:, do + C:do + NB][:, ::-1], data1=tib[:, do + C:do + NB][:, ::-1],
                                                                      initial=trb[:, do:do + 1], op0=ALU.mult, op1=ALU.add), r=[('a', si), ('ti', si), ('tr', si)], w=[('tr', si)])
                        d0 = s0 * NB
                        S.op('vector', lambda e: e.tensor_tensor(out=ab[:, do + C:do + NB], in0=trb[:, d0 + C:d0 + NB], in1=trb[:, do + C:do + NB], op=ALU.add),
                             r=[('tr', s0), ('tr', si), ('a', si)], w=[('a', si)])
                        S.op('vector', lambda e: e.tensor_tensor(out=Ylru[:, h * T:(h + 1) * T], in0=ab[:, do + C:do + NB], in1=Ylru[:, h * T:(h + 1) * T], op=ALU.mult),
                             r=[('a', si)], w=[('Yf', h)])

                units = [(h, d) for h in range(8) for d in range(2)]
                for s_ in range(17):
                    if s_ >= 1:
                        ph, pd = units[s_ - 1]
                        if pd == 1 and ph + 2 < 8:
                            stPa(ph + 2)
                    if s_ < 16:
                        stG(units[s_][0], units[s_][1], s_ % 3)
                    if s_ == 0:
                        stPa(1)
                        stPc(1, 0)
                        stPc(1, 1)
                    if s_ >= 1:
                        stD1(ph, pd, (s_ - 1) % 3)
                        stD2(ph, pd, (s_ - 1) % 3, (s_ - 2) % 3)
                        if pd == 1 and ph + 2 < 8:
                            stPc(ph + 2, 0)
                        if pd == 0 and 1 <= ph and ph + 1 < 8:
                            stPc(ph + 1, 1)
                    if s_ == 15:
                        S.op('gpsimd', lambda e: e.dma_start(out=k3(late['Wv'][:], 1024), in_=winv[:, :, 3072:4096]),
                             w=[('Wg', 0), ('Wg', 1), ('Wxs', 0), ('Wxs', 1), 'xa_c', 'xa_x'], dma=('wc', 'Wv'))
                        S.op('gpsimd', lambda e: e.dma_start(out=k3(late['Wu'][:], 1024), in_=winv[:, :, 2048:3072]),
                             w=['xa_x', ('xc', 0, 0), ('xc', 0, C), ('xcb', 0)], dma=('wc', 'Wu'))
            S.barrier()

        with contextlib.ExitStack() as _es:
            Wv = _es.enter_context(nc.sbuf_tensor("Wv", [128, 8 * 1024], BF16))
            late['Wv'] = Wv
            Wu = _es.enter_context(nc.sbuf_tensor("Wu", [128, 8 * 1024], BF16))
            late['Wu'] = Wu
            Wgb = _es.enter_context(nc.sbuf_tensor("Wgb", [128, 8 * 1024], BF16))
            wob = _es.enter_context(nc.sbuf_tensor("wob", [128, 16 * 1024], BF16))
            fgb = _es.enter_context(nc.sbuf_tensor("fgb", [128, D], F32))
            Bt = _es.enter_context(nc.sbuf_tensor("Bt", [128, 8 * 128], F32))
            wsTb = _es.enter_context(nc.sbuf_tensor("wsTb", [128, 1024], BF16))
            gv = _es.enter_context(nc.sbuf_tensor("gv", [128, D], F32))
            vhat = _es.enter_context(nc.sbuf_tensor("vhat", [128, 2 * D], BF16))
            gu = _es.enter_context(nc.sbuf_tensor("gu", [128, 8 * 256], F32))
            sgb = _es.enter_context(nc.sbuf_tensor("sgb", [128, 8 * 256], F32))
            Ysgu = _es.enter_context(nc.sbuf_tensor("Ysgu", [128, 8 * 256], BF16))
            xres = _es.enter_context(nc.sbuf_tensor("xres", [128, D], F32))
            xn = _es.enter_context(nc.sbuf_tensor("xn", [128, 2 * D], F32))
            st3 = _es.enter_context(nc.sbuf_tensor("st3", [128, 96], F32))
            S.op('gpsimd', lambda e: e.dma_start(out=wsTb[:], in_=wsT_d[:, :]), w=['wsTb'], dma=('wc', 'wsTb'))
            S.op('gpsimd', lambda e: e.dma_start(out=k3(Wgb[:], 1024), in_=winv[:, :, 4096:5120]), w=['Wgb'], dma=('wc', 'Wgb'))
            for hh in range(2):
                S.op('gpsimd', lambda e, hh=hh: e.dma_start(out=k3(wob[:, hh * 8192:(hh + 1) * 8192], 1024), in_=woutv[:, hh * 8:(hh + 1) * 8, :]),
                     w=[('wob', hh)], dma=('wc', 'wob', hh))
            ld(fgb[:], fgrep_d[:, :], ['fgb'])
            ld(xn[:, 0:D], sbrep_d[:, :], [('xn', 0)])

            def st_bt():
                def rs_mm(e):
                    i = None
                    for hf in range(2):
                        i = e.matmul(bank(6 + hf), lhsT=ones_b[:], rhs=wsTb[:, hf * 512:(hf + 1) * 512], start=True, stop=True)
                    return i
                S.op('tensor', rs_mm, r=['ones_b', 'wsTb'], w=[('ps', 6), ('ps', 7)])
                for g in range(8):
                    S.op('vector', lambda e, g=g: e.scalar_tensor_tensor(
                        out=Bt[:, g * 128:(g + 1) * 128], in0=ps[:, 6 * 512 + g * 128:6 * 512 + (g + 1) * 128],
                        scalar=par[:, P_LNB + g:P_LNB + g + 1], in1=xn[:, g * 128:(g + 1) * 128], op0=ALU.mult, op1=ALU.add),
                        r=[('ps', 6), ('ps', 7), ('xn', 0)], w=[('Bt', g)])

            ci = [0]

            def st_v(j, n):
                X0 = C + j * 256
                if True:
                    ci[0] += 1
                    so = (ci[0] % 4) * 16

                    def v_mm(e, n=n):
                        i = None
                        for hf in range(2):
                            for k in range(8):
                                i = e.matmul(bank(hf), lhsT=xmT[:, k * TT + X0 + n * 128:k * TT + X0 + (n + 1) * 128],
                                             rhs=Wv[:, k * 1024 + hf * 512:k * 1024 + (hf + 1) * 512], start=(k == 0), stop=(k == 7))
                        return i
                    S.op('tensor', v_mm, r=['Wv'], w=[('ps', 0), ('ps', 1)])
                    S.op('scalar', lambda e: e.activation(out=gv[:], in_=ps[:, 0:1024], func=AF.Gelu_apprx_tanh), r=[('ps', 0), ('ps', 1)], w=['gv'])
                    for hf in range(2):
                        S.op('vector', lambda e, hf=hf, so=so: e.bn_stats(out=st3[:, so + hf * 6:so + hf * 6 + 6], in_=gv[:, hf * 512:(hf + 1) * 512]),
                             r=['gv'], w=[('bst', so, hf)])
                    S.op('vector', lambda e, so=so: e.bn_aggr(out=st3[:, so + 12:so + 14], in_=st3[:, so:so + 12]), r=[('bst', so, 0), ('bst', so, 1)], w=[('mv', so)])
                    S.op('vector', lambda e, so=so: e.tensor_scalar(out=st3[:, so + 14:so + 15], in0=st3[:, so + 13:so + 14], scalar1=1e-5, scalar2=None, op0=ALU.add),
                         r=[('mv', so)], w=[('ve', so)])
                    S.op('gpsimd', lambda e, so=so: e.tensor_tensor(out=st3[:, so + 15:so + 16], in0=st3[:, so + 14:so + 15], in1=der[:, Q_NEGH:Q_NEGH + 1], op=ALU.pow),
                         r=[('ve', so)], w=[('rs', so)])
                    S.op('vector', lambda e, so=so, n=n: e.tensor_scalar(out=vhat[:, n * D:(n + 1) * D], in0=gv[:], scalar1=st3[:, so + 12:so + 13],
                                                                          scalar2=st3[:, so + 15:so + 16], op0=ALU.subtract, op1=ALU.mult),
                         r=['gv', ('mv', so), ('rs', so)], w=[('vhat', n)])

            def st_ug(j, which, pairs):
                X0 = C + j * 256
                for (Wt, wn, dst, dn, fn_) in (((Wu, 'Wu', gu, 'gu', AF.Gelu_apprx_tanh), (Wgb, 'Wgb', sgb, 'sgb', AF.Silu))[which],):
                    for gpair in pairs:
                        bk = 4 + gpair % 2

                        def ug_mm(e, gpair=gpair, bk=bk, Wt=Wt):
                            i = None
                            for g in (2 * gpair, 2 * gpair + 1):
                                co = (g % 2) * 256
                                for k in range(8):
                                    i = e.matmul(bank(bk, co, co + 256), lhsT=Wt[:, k * 1024 + g * 128:k * 1024 + (g + 1) * 128], rhs=xmT[:, k * TT + X0:k * TT + X0 + 256],
                                                 start=(k == 0), stop=(k == 7), skip_group_check=True)
                            return i
                        S.op('tensor', ug_mm, r=[wn], w=[('ps', bk)])
                        S.op('scalar', lambda e, gpair=gpair, bk=bk, dst=dst, fn_=fn_: e.activation(out=dst[:, gpair * 512:(gpair + 1) * 512], in_=bank(bk), func=fn_),
                             r=[('ps', bk)], w=[(dn, gpair)])

            def st_gg(j):
                S.op('vector', lambda e: e.tensor_tensor(out=gu[:], in0=gu[:], in1=sgb[:], op=ALU.mult),
                     r=[('gu', g) for g in range(4)] + [('sgb', g) for g in range(4)], w=[('gu', g) for g in range(4)])

            def st_mix(j, pairs, final):
                for gpair in pairs:
                    bk = 6 + gpair % 2

                    def mix_mm(e, gpair=gpair, bk=bk):
                        i = None
                        for g in (2 * gpair, 2 * gpair + 1):
                            co = (g % 2) * 256
                            for n in range(2):
                                i = e.matmul(bank(bk, co + n * 128, co + (n + 1) * 128), lhsT=vhat[:, n * D + g * 128:n * D + (g + 1) * 128], rhs=wsTb[:, g * 128:(g + 1) * 128],
                                             start=True, stop=True, skip_group_check=True)
                        return i
                    S.op('tensor', mix_mm, r=[('vhat', 0), ('vhat', 1), 'wsTb'], w=[('ps', bk)])
                    for g in (2 * gpair, 2 * gpair + 1):
                        co = (g % 2) * 256
                        for n in range(2):
                            S.op('vector', lambda e, g=g, bk=bk, co=co, n=n: e.scalar_tensor_tensor(
                                out=sgb[:, g * 256 + n * 128:g * 256 + (n + 1) * 128], in0=bank(bk, co + n * 128, co + (n + 1) * 128),
                                scalar=par[:, P_LNG + g:P_LNG + g + 1], in1=Bt[:, g * 128:(g + 1) * 128], op0=ALU.mult, op1=ALU.add),
                                r=[('ps', bk), 'p_lng', ('Bt', g)], w=[('sgb', gpair)])
                if final:
                    S.op('vector', lambda e: e.tensor_tensor(out=Ysgu[:], in0=sgb[:], in1=gu[:], op=ALU.mult),
                         r=[('gu', g) for g in range(4)] + [('sgb', g) for g in range(4)], w=['Ysgu'])

            def st_o1(j, n):
                tg = j * 2 + n
                ob = tg % 2
                so = 64 + (tg % 4) * 4
                pb = 2 if n == 0 else 0
                S.op('sync', lambda e: e.dma_start(out=xres[:], in_=x_d[tg * 128:(tg + 1) * 128, :]), w=['xres'], dma=('xr', 0))

                def o_mm(e):
                    i = None
                    for hf in range(2):
                        for kc in range(16):
                            if kc < 8:
                                lt = Ylru[:, kc * T + j * 256 + n * 128:kc * T + j * 256 + (n + 1) * 128]
                            else:
                                lt = Ysgu[:, (kc - 8) * 256 + n * 128:(kc - 8) * 256 + (n + 1) * 128]
                            i = e.matmul(bank(pb + hf), lhsT=lt, rhs=wob[:, kc * 1024 + hf * 512:kc * 1024 + (hf + 1) * 512], start=(kc == 0), stop=(kc == 15))
                    return i
                S.op('tensor', o_mm, r=['Ysgu', ('wob', 0), ('wob', 1)], w=[('ps', pb), ('ps', pb + 1)])
                S.op('vector', lambda e: e.tensor_tensor(out=xn[:, ob * D:(ob + 1) * D], in0=ps[:, pb * 512:(pb + 2) * 512], in1=gxb[:], op=ALU.mult),
                     r=[('ps', pb), ('ps', pb + 1), 'gxb'] + [('Bt', g) for g in range(8)], w=[('xn', ob)])
                S.op('vector', lambda e: e.tensor_tensor(out=xn[:, ob * D:(ob + 1) * D], in0=xn[:, ob * D:(ob + 1) * D], in1=xres[:], op=ALU.add),
                     r=[('xn', ob), 'xres'], w=[('xn', ob)])
                S.op('scalar', lambda e: e.activation(out=xres[:], in_=xn[:, ob * D:(ob + 1) * D], func=AF.Square, accum_out=st3[:, so:so + 1]),
                     r=[('xn', ob)], w=['xres', ('q0', so)])

            def st_o2(j, n):
                tg = j * 2 + n
                so = 64 + (tg % 4) * 4
                S.op('vector', lambda e: e.tensor_scalar(out=st3[:, so + 1:so + 2], in0=st3[:, so:so + 1], scalar1=1.0 / D, scalar2=1e-6, op0=ALU.mult, op1=ALU.add),
                     r=[('q0', so)], w=[('q1', so)])
                S.op('gpsimd', lambda e: e.tensor_tensor(out=st3[:, so + 2:so + 3], in0=st3[:, so + 1:so + 2], in1=der[:, Q_NEGH:Q_NEGH + 1], op=ALU.pow),
                     r=[('q1', so)], w=[('q2', so)])

            def st_o3(j, n):
                tg = j * 2 + n
                ob = tg % 2
                so = 64 + (tg % 4) * 4
                S.op('vector', lambda e: e.scalar_tensor_tensor(out=xn[:, ob * D:(ob + 1) * D], in0=xn[:, ob * D:(ob + 1) * D], scalar=st3[:, so + 2:so + 3],
                                                                in1=fgb[:], op0=ALU.mult, op1=ALU.mult),
                     r=[('xn', ob), ('q2', so), 'fgb'], w=[('xn', ob)])
                S.op('sync', lambda e: e.dma_start(out=out_d[tg * 128:(tg + 1) * 128, :], in_=xn[:, ob * D:(ob + 1) * D]), r=[('xn', ob)], dma=('st', ob))

            for j in range(8):
                st_v(j, 0)
                st_ug(j, 0, (0, 1))
                st_ug(j, 0, (2, 3))
                st_v(j, 1)
                st_ug(j, 1, (0, 1, 2, 3))
                st_gg(j)
                if j == 0:
                    st_bt()
                if j >= 1:
                    st_o1(j - 1, 0)
                st_mix(j, (0, 1), False)
                if j >= 1:
                    st_o1(j - 1, 1)
                st_mix(j, (2, 3), True)
                if j >= 1:
                    st_o2(j - 1, 0)
                    st_o2(j - 1, 1)
                    st_o3(j - 1, 0)
                    st_o3(j - 1, 1)
            for n in range(2):
                st_o1(7, n)
            for n in range(2):
                st_o2(7, n)
            for n in range(2):
                st_o3(7, n)

        keys = list(S.cnt.keys())
        with contextlib.ExitStack() as es:
            sems = {}
            for i, k in enumerate(keys):
                sems[k] = es.enter_context(nc.semaphore("s%d" % i))
            block = es.enter_context(nc.Block())
            S.emit(nc, block, sems)
    return nc


_NC_CACHE = {}


def kernel(x, c, ctx, c_ctx, ada_w, ada_b, norm_g, w_in, conv_w, conv_b, lru_wa, lru_ba, lru_wx, lru_bx,
           lru_lambda, sgu_ln_g, sgu_ln_b, sgu_w, sgu_b, w_out, final_g):
    f = lambda a: np.ascontiguousarray(np.asarray(a, dtype=np.float32))
    x = f(x); c = f(c); ctx = f(ctx); c_ctx = f(c_ctx)
    ada_w0 = f(ada_w[0]); ada_b0 = f(ada_b[0]); ng = f(norm_g[0]); w_in0 = f(w_in[0]); w_out0 = f(w_out[0])

    def colT(v, nchunk):
        return f(np.asarray(v, dtype=np.float32).reshape(nchunk, 128).T)
    adabT = colT(ada_b0, 24)
    bg_rep = f(np.broadcast_to(ada_b0[2048:3072][None, :], (128, 1024)))
    fg_rep = f(np.broadcast_to(np.asarray(final_g, dtype=np.float32)[None, :], (128, 1024)))
    ngT = colT(ng, 8)
    cwT = f(np.asarray(conv_w[0], dtype=np.float32).reshape(4, 8, 128).transpose(2, 1, 0).reshape(128, 32))
    cbT = colT(conv_b[0], 8)
    wa = np.asarray(lru_wa[0], dtype=np.float32)
    wx = np.asarray(lru_wx[0], dtype=np.float32)
    wg = f(np.stack([wa, wx], 0).transpose(3, 0, 1, 2, 4).reshape(128, 4096))
    baT = f(np.asarray(lru_ba[0], dtype=np.float32).transpose(2, 0, 1).reshape(128, 16))
    bxT = f(np.asarray(lru_bx[0], dtype=np.float32).transpose(2, 0, 1).reshape(128, 16))
    lamT = f(np.asarray(lru_lambda[0], dtype=np.float32).reshape(2, 8, 128).transpose(2, 0, 1).reshape(128, 16))
    lngT = colT(sgu_ln_g[0], 8)
    lnbT = colT(sgu_ln_b[0], 8)
    wsT = f(np.asarray(sgu_w[0], dtype=np.float32).transpose(2, 0, 1).reshape(128, 1024))
    sb_rep = f(np.broadcast_to(np.asarray(sgu_b[0], dtype=np.float32).reshape(1, 1024), (128, 1024)))

    if 'nc' not in _NC_CACHE:
        _NC_CACHE['nc'] = build_nc()
    nc = _NC_CACHE['nc']
    in_maps = []
    for b in range(NCORES):
        cT = f(np.stack([c[b].reshape(8, 128).T, c_ctx.reshape(8, 128).T], axis=2).reshape(128, 16))
        in_maps.append({
            "x": x[b], "ctx": ctx[b], "cT": cT, "ada_w": ada_w0, "adabT": adabT, "bg_rep": bg_rep, "fg_rep": fg_rep,
            "ngT": ngT, "w_in": w_in0, "w_out": w_out0, "cwT": cwT, "cbT": cbT, "wg": wg, "baT": baT, "bxT": bxT,
            "lamT": lamT, "lngT": lngT, "lnbT": lnbT, "wsT": wsT, "sb_rep": sb_rep,
        })
    res = run_bass_kernel_spmd(nc, in_maps, core_ids=list(range(NCORES)))
    return np.stack([np.asarray(r["out"], dtype=np.float32) for r in res.results], axis=0)
```

```python
import contextlib
import numpy as np
import concourse.bass as bass
import concourse.mybir as mybir
from concourse.bass_utils import run_bass_kernel_spmd
from concourse.alu_op_type import AluOpType as ALU

F32 = mybir.dt.float32
BF16 = mybir.dt.bfloat16
AF = mybir.ActivationFunctionType

D = 1024
T = 2048
C = 256
TT = T + C
NCORES = 8
ENGS = ['sync', 'scalar', 'vector', 'gpsimd', 'tensor']


class Sched:
    def __init__(self):
        self.all = []
        self.cnt = {}
        self.lastw = {}
        self.readers = {}

    def op(self, eng, fn, r=(), w=(), dma=None):
        key = ('d', dma) if dma else ('c', eng)
        inc = 16 if dma else 1
        self.cnt[key] = self.cnt.get(key, 0) + inc
        tok = (key, self.cnt[key])
        deps = {}

        def add(k, v):
            if deps.get(k, 0) < v:
                deps[k] = v
        xs_ = [s for s in r if isinstance(s, tuple) and s[0] == 'ps']
        r = [s for s in r if not (isinstance(s, tuple) and s[0] == 'ps')]
        for s in r:
            t = self.lastw.get(s)
            if t: add(t[0], t[1])
        for s in w:
            t = self.lastw.get(s)
            if t: add(t[0], t[1])
            for k, v in self.readers.get(s, {}).items():
                add(k, v)
        for s in xs_:
            t = self.lastw.get(s)
            if t and not (t[0] == key and t[2]):
                add(t[0], t[1])
            for k, v in self.readers.get(s, {}).items():
                add(k, v)
        for s in r:
            rd = self.readers.setdefault(s, {})
            if rd.get(key, 0) < tok[1]:
                rd[key] = tok[1]
        for s in w:
            self.lastw[s] = (tok[0], tok[1], False)
            self.readers[s] = {}
        for s in xs_:
            self.lastw[s] = (tok[0], tok[1], True)
            self.readers[s] = {}
        self.all.append((eng, fn, deps, key, inc))
        return tok

    def barrier(self):
        allc = dict(self.cnt)
        for e in ENGS:
            self.all.append((e, None, dict(allc), None, 0))
        self.lastw = {}
        self.readers = {}

    def emit(self, nc, block, sems, limit=None):
        ops = self.all if limit is None else self.all[:limit]
        final = {}
        for (eng, fn, deps, key, inc) in ops:
            if fn is not None:
                final[key] = final.get(key, 0) + inc
        for eng in ENGS:
            q = [o for o in ops if o[0] == eng]

            def body(e, q=q, eng=eng):
                waited = {}
                for (_, fn, deps, key, inc) in q:
                    for k, v in deps.items():
                        if eng == 'tensor' and k == ('c', 'tensor'):
                            continue
                        if waited.get(k, 0) < v:
                            e.wait_ge(sems[k], v)
                            waited[k] = v
                    if fn is not None:
                        ins = fn(e)
                        ins.then_inc(sems[key], inc)
                for k, v in final.items():
                    if waited.get(k, 0) < v:
                        e.wait_ge(sems[k], v)
            getattr(block, eng)(body)


def build_nc():
    nc = bass.Bass("TRN2", target_bir_lowering=False)

    def din(name, shape):
        return nc.dram_tensor(name, list(shape), F32, kind="ExternalInput").ap()
    x_d = din("x", [T, D])
    ctx_d = din("ctx", [C, D])
    cT_d = din("cT", [128, 16])
    adaw_d = din("ada_w", [D, 3 * D])
    adabT_d = din("adabT", [128, 24])
    bgrep_d = din("bg_rep", [128, D])
    fgrep_d = din("fg_rep", [128, D])
    ngT_d = din("ngT", [128, 8])
    win_d = din("w_in", [D, 5 * D])
    wout_d = din("w_out", [2 * D, D])
    cwT_d = din("cwT", [128, 32])
    cbT_d = din("cbT", [128, 8])
    wg_d = din("wg", [128, 4096])
    baT_d = din("baT", [128, 16])
    bxT_d = din("bxT", [128, 16])
    lamT_d = din("lamT", [128, 16])
    lngT_d = din("lngT", [128, 8])
    lnbT_d = din("lnbT", [128, 8])
    wsT_d = din("wsT", [128, 1024])
    sbrep_d = din("sb_rep", [128, 1024])
    out_d = nc.dram_tensor("out", [T, D], F32, kind="ExternalOutput").ap()

    S = Sched()
    late = {}
    winv = win_d.rearrange("(k p) n -> p k n", p=128)
    woutv = wout_d.rearrange("(k p) n -> p k n", p=128)
    adawv = adaw_d.rearrange("(k p) n -> p k n", p=128)

    P_CT, P_ADAB, P_NG, P_CW, P_CB, P_BA, P_BX, P_LAM, P_LNG, P_LNB = 0, 16, 40, 48, 80, 88, 104, 120, 136, 144
    NPAR = 152
    Q_CS, Q_MODX, Q_MODC, Q_GMX, Q_GMC, Q_E1, Q_SP, Q_CL, Q_CH, Q_HBA, Q_HBX, Q_NEGH = 0, 16, 32, 48, 56, 64, 80, 96, 112, 128, 144, 160
    NDER = 168
    NB = TT

    def k3(t, n):
        return t.rearrange("p (k n) -> p k n", n=n)

    with contextlib.ExitStack() as _es:
        par = _es.enter_context(nc.sbuf_tensor("par", [128, NPAR], F32))
        der = _es.enter_context(nc.sbuf_tensor("der", [128, NDER], F32))
        stat = _es.enter_context(nc.sbuf_tensor("stat", [128, 4 * 24], F32))
        ident = _es.enter_context(nc.sbuf_tensor("ident", [128, 128], F32))
        ones_b = _es.enter_context(nc.sbuf_tensor("ones_b", [128, 128], BF16))
        csbf = _es.enter_context(nc.sbuf_tensor("csbf", [128, 16], BF16))
        gxb = _es.enter_context(nc.sbuf_tensor("gxb", [128, D], F32))
        xmT = _es.enter_context(nc.sbuf_tensor("xmT", [128, 8 * TT], BF16))
        Ylru = _es.enter_context(nc.sbuf_tensor("Ylru", [128, 8 * T], BF16))
        ps = _es.enter_context(nc.psum_tensor("ps", [128, 4096], F32))

        def bank(i, c0=0, c1=512):
            return ps[:, i * 512 + c0:i * 512 + c1]

        def ld(dst, src, w, r=()):
            S.op('sync', lambda e: e.dma_start(out=dst, in_=src), r=r, w=w, dma=('par', w[0]))

        with contextlib.ExitStack() as _es:
            Wg = _es.enter_context(nc.sbuf_tensor("Wg", [128, 4096], BF16))
            Wxs = _es.enter_context(nc.sbuf_tensor("Wxs", [128, 2 * 1024], BF16))
            xa_c = _es.enter_context(nc.sbuf_tensor("xa_c", [128, C + 3], F32))
            xa_x = _es.enter_context(nc.sbuf_tensor("xa_x", [128, T + 3], F32))
            xcA = _es.enter_context(nc.sbuf_tensor("xcA", [128, NB], F32))
            xcbA = _es.enter_context(nc.sbuf_tensor("xcbA", [128, NB], BF16))
            ibc = [0]
            def wxa_load(h):
                S.op('gpsimd', lambda e, h=h: e.dma_start(out=k3(Wxs[:, (h % 2) * 1024:(h % 2 + 1) * 1024], 128), in_=winv[:, :, h * 128:(h + 1) * 128]),
                     w=[('Wxs', h % 2)], dma=('wc', 'Wxs', h % 2))

            xcs = [xcA, None]
            xcbs = [xcbA, None]

            def stPa(h):
                hp = h % 2
                for j in range(5):
                    c0, c1 = (0, C) if j == 0 else (C + (j - 1) * 512, C + j * 512)
                    n = c1 - c0
                    bk = 6 + (ibc[0] % 2)
                    ibc[0] += 1

                    def xa_mm(e, c0=c0, c1=c1, n=n, bk=bk):
                        i = None
                        for k in range(8):
                            i = e.matmul(bank(bk, 0, n), lhsT=Wxs[:, hp * 1024 + k * 128:hp * 1024 + (k + 1) * 128],
                                         rhs=xmT[:, k * TT + c0:k * TT + c1], start=(k == 0), stop=(k == 7))
                        return i
                    S.op('tensor', xa_mm, r=[('Wxs', hp)] + [('xmT', g2, k) for g2 in range(c0 // 256, (c1 + 255) // 256) for k in range(8)], w=[('ps', bk)])
                    if j == 0:
                        S.op('scalar', lambda e, bk=bk: e.activation(out=xa_c[:, 1:1 + C], in_=bank(bk, 0, C), func=AF.Identity), r=[('ps', bk)], w=['xa_c'])
                    else:
                        S.op('scalar', lambda e, bk=bk, j=j: e.activation(out=xa_x[:, 1 + (j - 1) * 512:1 + j * 512], in_=bank(bk), func=AF.Identity), r=[('ps', bk)], w=['xa_x'])
                if h + 2 < 8:
                    wxa_load(h + 2)

            def stPc(h, part):
                hp = h % 2
                xo = hp * NB
                (src, sl, L, o0) = ((xa_c, 'xa_c', C, 0), (xa_x, 'xa_x', T, C))[part]
                S.op('vector', lambda e: e.tensor_scalar(
                    out=xcs[hp][:, o0:o0 + L], in0=src[:, 0:L], scalar1=par[:, P_CW + h * 4:P_CW + h * 4 + 1], scalar2=par[:, P_CB + h:P_CB + h + 1],
                    op0=ALU.mult, op1=ALU.add), r=[sl, 'p_cw', 'p_cb'], w=[('xc', hp, o0)])
                for tp in range(1, 4):
                    S.op('vector', lambda e, tp=tp: e.scalar_tensor_tensor(
                        out=xcs[hp][:, o0:o0 + L], in0=src[:, tp:tp + L], scalar=par[:, P_CW + h * 4 + tp:P_CW + h * 4 + tp + 1], in1=xcs[hp][:, o0:o0 + L],
                        op0=ALU.mult, op1=ALU.add), r=[sl, 'p_cw', ('xc', hp, o0)], w=[('xc', hp, o0)])
                if part == 1:
                    S.op('vector', lambda e: e.tensor_copy(out=xcbs[hp][:], in_=xcs[hp][:]), r=[('xc', hp, 0), ('xc', hp, C)], w=[('xcb', hp)])


            with contextlib.ExitStack() as _es:
                adaW = _es.enter_context(nc.sbuf_tensor("adaW", [128, 8 * 3072], BF16))
                Wgv = _es.enter_context(nc.sbuf_tensor("Wga", [128, 8 * 1024], BF16))
                xin = _es.enter_context(nc.sbuf_tensor("xin", [128, 4 * D], F32))
                xs = _es.enter_context(nc.sbuf_tensor("xs", [128, 2 * D], F32))
                junk = _es.enter_context(nc.sbuf_tensor("junk", [128, D], BF16))
                csb = _es.enter_context(nc.sbuf_tensor("csb", [128, 8 * 128], BF16))
                bgrep = _es.enter_context(nc.sbuf_tensor("bgrep", [128, 1024], F32))
                ld(par[:, P_CT:P_CT + 16], cT_d[:, :], ['p_ct'])
                ld(par[:, P_ADAB:P_ADAB + 24], adabT_d[:, :], ['p_adab'])
                ld(par[:, P_NG:P_NG + 8], ngT_d[:, :], ['p_ng'])
                for sl in range(2):
                    S.op('gpsimd', lambda e, sl=sl: e.dma_start(out=k3(adaW[:], 3072)[:, :, sl * 1024:(sl + 1) * 1024], in_=adawv[:, :, sl * 1024:(sl + 1) * 1024]),
                         w=[('adaW', sl)], dma=('wc', 'adaW', sl))
                S.op('gpsimd', lambda e: e.dma_start(out=k3(Wgv[:], 1024), in_=winv[:, :, 1024:2048]), w=['Wgv'], dma=('wc', 'Wgv'))

                def late_loads():
                    S.op('gpsimd', lambda e: e.dma_start(out=k3(adaW[:], 3072)[:, :, 2048:3072], in_=adawv[:, :, 2048:3072]), w=[('adaW', 2)], dma=('wc', 'adaW', 2))
                    for hh in range(2):
                        S.op('gpsimd', lambda e, hh=hh: e.dma_start(out=Wg[:, hh * 2048:(hh + 1) * 2048], in_=wg_d[:, hh * 2048:(hh + 1) * 2048]),
                             w=[('Wg', hh)], dma=('wc', 'Wg', hh))
                    wxa_load(0)
                    wxa_load(1)
                ld(par[:, P_LAM:P_LAM + 16], lamT_d[:, :], ['p_lam'])
                ld(par[:, P_BA:P_BA + 16], baT_d[:, :], ['p_ba'])
                ld(par[:, P_BX:P_BX + 16], bxT_d[:, :], ['p_bx'])
                ld(par[:, P_CW:P_CW + 32], cwT_d[:, :], ['p_cw'])
                ld(par[:, P_CB:P_CB + 8], cbT_d[:, :], ['p_cb'])
                ld(par[:, P_LNG:P_LNG + 8], lngT_d[:, :], ['p_lng'])
                ld(par[:, P_LNB:P_LNB + 8], lnbT_d[:, :], ['p_lnb'])
                ld(bgrep[:], bgrep_d[:, :], ['bgrep'])

                S.op('gpsimd', lambda e: e.memset(ident[:], 0.0), w=['ident'])
                S.op('gpsimd', lambda e: e.affine_select(out=ident[:], in_=ident[:], pattern=[[-1, 128]], compare_op=ALU.not_equal,
                                                         fill=1.0, base=0, channel_multiplier=1), r=['ident'], w=['ident'])
                S.op('gpsimd', lambda e: e.memset(ones_b[:], 1.0), w=['ones_b'])
                S.op('gpsimd', lambda e: e.memset(xa_c[:], 0.0), w=['xa_c'])
                S.op('gpsimd', lambda e: e.memset(xa_x[:], 0.0), w=['xa_x'])
                S.op('gpsimd', lambda e: e.memset(der[:, Q_NEGH:Q_NEGH + 1], -0.5), w=['negh'])

                S.op('scalar', lambda e: e.activation(out=csbf[:], in_=par[:, P_CT:P_CT + 16], func=AF.Silu), r=['p_ct'], w=['cs'])
                S.op('scalar', lambda e: e.activation(out=der[:, Q_E1:Q_E1 + 16], in_=par[:, P_LAM:P_LAM + 16], func=AF.Exp, scale=-1.0), r=['p_lam'], w=['e1'])
                S.op('scalar', lambda e: e.activation(out=der[:, Q_SP:Q_SP + 16], in_=der[:, Q_E1:Q_E1 + 16], func=AF.Ln, bias=1.0, scale=1.0), r=['e1'], w=['sp'])
                S.op('vector', lambda e: e.tensor_scalar(out=der[:, Q_CL:Q_CL + 16], in0=der[:, Q_SP:Q_SP + 16], scalar1=-8.0, scalar2=None, op0=ALU.mult), r=['sp'], w=['cl'])
                S.op('vector', lambda e: e.tensor_scalar(out=der[:, Q_CH:Q_CH + 16], in0=der[:, Q_SP:Q_SP + 16], scalar1=-4.0, scalar2=None, op0=ALU.mult), r=['sp'], w=['ch'])
                S.op('vector', lambda e: e.tensor_scalar(out=der[:, Q_HBA:Q_HBA + 16], in0=par[:, P_BA:P_BA + 16], scalar1=0.5, scalar2=None, op0=ALU.mult), r=['p_ba'], w=['hba'])
                S.op('vector', lambda e: e.tensor_scalar(out=der[:, Q_HBX:Q_HBX + 16], in0=par[:, P_BX:P_BX + 16], scalar1=0.5, scalar2=None, op0=ALU.mult), r=['p_bx'], w=['hbx'])
                for k in range(8):
                    S.op('vector', lambda e, k=k: e.tensor_scalar(out=csb[:, k * 128:(k + 1) * 128], in0=ones_b[:], scalar1=csbf[:, 2 * k:2 * k + 1],
                                                                  scalar2=None, op0=ALU.mult), r=['ones_b', 'cs'], w=[('csb', k)])

                def ada_stage():
                    def ada_mm(e):
                        i = None
                        for fc in range(16):
                            for k in range(8):
                                i = e.matmul(ps[:, 7 * 512 + fc * 2:7 * 512 + fc * 2 + 2], lhsT=adaW[:, k * 3072 + fc * 128:k * 3072 + (fc + 1) * 128],
                                             rhs=csbf[:, 2 * k:2 * k + 2], start=(k == 0), stop=(k == 7), skip_group_check=True)
                        return i
                    S.op('tensor', ada_mm, r=[('adaW', 0), ('adaW', 1), 'cs'], w=[('ps', 7)])
                    modps = ps[:, 7 * 512:7 * 512 + 32].rearrange("p (f v) -> p f v", v=2)
                    S.op('vector', lambda e: e.tensor_tensor(out=der[:, Q_MODX:Q_MODX + 16], in0=modps[:, :, 0], in1=par[:, P_ADAB:P_ADAB + 16], op=ALU.add),
                         r=[('ps', 7), 'p_adab'], w=['modx'])
                    S.op('vector', lambda e: e.tensor_tensor(out=der[:, Q_MODC:Q_MODC + 16], in0=modps[:, :, 1], in1=par[:, P_ADAB:P_ADAB + 16], op=ALU.add),
                         r=[('ps', 7), 'p_adab'], w=['modc'])
                    S.op('vector', lambda e: e.scalar_tensor_tensor(out=der[:, Q_GMX:Q_GMX + 8], in0=der[:, Q_MODX + 8:Q_MODX + 16], scalar=1.0, in1=par[:, P_NG:P_NG + 8],
                                                                    op0=ALU.add, op1=ALU.mult), r=['modx', 'p_ng'], w=['gmx'])
                    S.op('vector', lambda e: e.scalar_tensor_tensor(out=der[:, Q_GMC:Q_GMC + 8], in0=der[:, Q_MODC + 8:Q_MODC + 16], scalar=1.0, in1=par[:, P_NG:P_NG + 8],
                                                                    op0=ALU.add, op1=ALU.mult), r=['modc', 'p_ng'], w=['gmc'])

                def gx_stage():
                    def gx_mm(e):
                        i = None
                        for hf in range(2):
                            for k in range(8):
                                i = e.matmul(bank(5 + hf), lhsT=csb[:, k * 128:(k + 1) * 128], rhs=adaW[:, k * 3072 + 2048 + hf * 512:k * 3072 + 2048 + (hf + 1) * 512],
                                             start=(k == 0), stop=(k == 7))
                        return i
                    S.op('tensor', gx_mm, r=[('adaW', 2)] + [('csb', k) for k in range(8)], w=[('ps', 5), ('ps', 6)])
                    S.op('vector', lambda e: e.tensor_tensor(out=gxb[:], in0=ps[:, 5 * 512:7 * 512], in1=bgrep[:], op=ALU.add),
                         r=[('ps', 5), ('ps', 6), 'bgrep'], w=['gxb'])

                def stA(t):
                    b4 = t % 4
                    src = ctx_d[t * 128:(t + 1) * 128, :] if t < 2 else x_d[(t - 2) * 128:(t - 1) * 128, :]
                    S.op('sync', lambda e: e.dma_start(out=xin[:, b4 * D:(b4 + 1) * D], in_=src), w=[('xin', b4)], dma=('xl', b4))
                    S.op('scalar', lambda e: e.activation(out=junk[:], in_=xin[:, b4 * D:(b4 + 1) * D], func=AF.Square, accum_out=stat[:, t:t + 1]),
                         r=[('xin', b4)], w=['junk', ('ssq', t)])

                def stB(t):
                    S.op('vector', lambda e: e.tensor_scalar(out=stat[:, 24 + t:25 + t], in0=stat[:, t:t + 1], scalar1=1.0 / D, scalar2=1e-6, op0=ALU.mult, op1=ALU.add),
                         r=[('ssq', t)], w=[('ms', t)])
                    S.op('gpsimd', lambda e: e.tensor_tensor(out=stat[:, 48 + t:49 + t], in0=stat[:, 24 + t:25 + t], in1=der[:, Q_NEGH:Q_NEGH + 1], op=ALU.pow),
                         r=[('ms', t), 'negh'], w=[('rstd', t)])

                gcount = [0]

                def stC(t):
                    b4 = t % 4
                    b3 = t % 2
                    gi = t // 2
                    tt = t % 2
                    S.op('vector', lambda e: e.tensor_scalar(out=xs[:, b3 * D:(b3 + 1) * D], in0=xin[:, b4 * D:(b4 + 1) * D], scalar1=stat[:, 48 + t:49 + t], scalar2=None, op0=ALU.mult),
                         r=[('xin', b4), ('rstd', t)], w=[('xs', b3)])

                    def tr_fn(e):
                        i = None
                        for k in range(8):
                            c0 = (k // 2) * 512 + (k % 2) * 256 + tt * 128
                            i = e.transpose(out=ps[:, c0:c0 + 128], in_=xs[:, b3 * D + k * 128:b3 * D + (k + 1) * 128], identity=ident[:])
                        return i
                    S.op('tensor', tr_fn, r=[('xs', b3), 'ident'], w=[('ps', 0), ('ps', 1), ('ps', 2), ('ps', 3)])
                    if tt == 1:
                        if gi == 0:
                            ada_stage()
                        gmo, sho = (Q_GMC, Q_MODC) if gi == 0 else (Q_GMX, Q_MODX)
                        gmn, shn = ('gmc', 'modc') if gi == 0 else ('gmx', 'modx')
                        for k in range(8):
                            c0 = (k // 2) * 512 + (k % 2) * 256
                            dst = xmT[:, k * TT + gi * 256:k * TT + (gi + 1) * 256]
                            if k < 4:
                                S.op('scalar', lambda e, c0=c0, dst=dst, k=k: e.activation(
                                    out=dst, in_=ps[:, c0:c0 + 256], func=AF.Identity, scale=der[:, gmo + k:gmo + k + 1], bias=der[:, sho + k:sho + k + 1]),
                                    r=[('ps', k // 2), gmn, shn], w=[('xmT', gi, k)])
                            else:
                                S.op('vector', lambda e, c0=c0, dst=dst, k=k: e.tensor_scalar(
                                    out=dst, in0=ps[:, c0:c0 + 256], scalar1=der[:, gmo + k:gmo + k + 1], scalar2=der[:, sho + k:sho + k + 1], op0=ALU.mult, op1=ALU.add),
                                    r=[('ps', k // 2), gmn, shn], w=[('xmT', gi, k)])
                        if gi == 8:
                            gx_stage()
                        gjobs = []
                        if gi >= 3:
                            gjobs.append(((gi - 3) // 2, range(0, 4) if gi % 2 == 1 else range(4, 8)))
                        if gi == 8:
                            gjobs.append((3, range(8)))
                        for (j, heads) in gjobs:
                            if j == 3:
                                stPa(0)
                                stPc(0, 0)
                                stPc(0, 1)
                            for h in heads:
                                bk = 4 + gcount[0] % 4
                                gcount[0] += 1

                                def g_mm(e, h=h, j=j, bk=bk):
                                    i = None
                                    for k in range(8):
                                        i = e.matmul(bank(bk), lhsT=Wgv[:, k * 1024 + h * 128:k * 1024 + (h + 1) * 128],
                                                     rhs=xmT[:, k * TT + C + j * 512:k * TT + C + (j + 1) * 512], start=(k == 0), stop=(k == 7))
                                    return i
                                S.op('tensor', g_mm, r=['Wgv'] + [('xmT', g2, k) for g2 in (2 * j + 1, 2 * j + 2) for k in range(8)], w=[('ps', bk)])
                                S.op('scalar', lambda e, h=h, j=j, bk=bk: e.activation(out=Ylru[:, h * T + j * 512:h * T + (j + 1) * 512], in_=bank(bk), func=AF.Silu),
                                     r=[('ps', bk)], w=[('Y', h, j)])

                for s_ in range(18 + 2):
                    if s_ == 13:
                        late_loads()
                    if s_ < 18:
                        stA(s_)
                    if 0 <= s_ - 1 < 18:
                        stB(s_ - 1)
                    if 0 <= s_ - 2 < 18:
                        stC(s_ - 2)
            S.barrier()

            with contextlib.ExitStack() as _es:
                xcB = _es.enter_context(nc.sbuf_tensor("xcB", [128, NB], F32))
                xcbB = _es.enter_context(nc.sbuf_tensor("xcbB", [128, NB], BF16))
                xcs[1] = xcB
                xcbs[1] = xcbB
                trb = _es.enter_context(nc.sbuf_tensor("trb", [128, 3 * NB], F32))
                tib = _es.enter_context(nc.sbuf_tensor("tib", [128, 3 * NB], F32))
                ab = _es.enter_context(nc.sbuf_tensor("ab", [128, 3 * NB], F32))
                gpc = [0]

                def stG(h, d, si):
                    hp = h % 2
                    xo = hp * NB
                    do = si * NB
                    col = d * 8 + h
                    for gate, dstb, hbo, nm in ((0, trb, Q_HBA, 'tr'), (1, tib, Q_HBX, 'ti')):
                        woff = ((gate * 2 + d) * 8 + h) * 128
                        for (c0, c1) in ((0, C), (C, C + 1024), (C + 1024, NB)):
                            pp = gpc[0] % 3
                            gpc[0] += 1
                            n = c1 - c0

                            def gate_mm(e, woff=woff, c0=c0, c1=c1, pp=pp):
                                i = None
                                for q0 in range(c0, c1, 512):
                                    q1 = min(q0 + 512, c1)
                                    i = e.matmul(ps[:, pp * 1024 + (q0 - c0):pp * 1024 + (q1 - c0)], lhsT=Wg[:, woff:woff + 128], rhs=xcbs[hp][:, q0:q1],
                                                 start=True, stop=True)
                                return i
                            S.op('tensor', gate_mm, r=[('Wg', 0), ('Wg', 1), ('xcb', hp)], w=[('ps', 2 * pp), ('ps', 2 * pp + 1)])
                            S.op('scalar', lambda e, dstb=dstb, c0=c0, c1=c1, pp=pp, n=n, hbo=hbo: e.activation(
                                out=dstb[:, do + c0:do + c1], in_=ps[:, pp * 1024:pp * 1024 + n], func=AF.Tanh, scale=0.5, bias=der[:, hbo + col:hbo + col + 1]),
                                r=[('ps', 2 * pp), ('ps', 2 * pp + 1), 'hba', 'hbx'], w=[(nm, si)])
                    S.op('scalar', lambda e: e.activation(out=ab[:, do:do + NB], in_=trb[:, do:do + NB], func=AF.Exp,
                                                          scale=der[:, Q_CH + col:Q_CH + col + 1], bias=der[:, Q_CH + col:Q_CH + col + 1]),
                         r=[('tr', si), 'ch'], w=[('a', si)])
                    S.op('scalar', lambda e: e.activation(out=trb[:, do:do + NB], in_=trb[:, do:do + NB], func=AF.Exp,
                                                          scale=der[:, Q_CL + col:Q_CL + col + 1], bias=der[:, Q_CL + col:Q_CL + col + 1]),
                         r=[('tr', si), 'cl'], w=[('tr', si)])
                    S.op('scalar', lambda e: e.activation(out=trb[:, do:do + NB], in_=trb[:, do:do + NB], func=AF.Sqrt, scale=-0.25, bias=0.25),
                         r=[('tr', si)], w=[('tr', si)])

                def stD1(h, d, si):
                    hp = h % 2
                    xo = hp * NB
                    do = si * NB
                    S.op('vector', lambda e: e.scalar_tensor_tensor(out=tib[:, do:do + NB], in0=tib[:, do:do + NB], scalar=1.0, in1=xcs[hp][:], op0=ALU.add, op1=ALU.mult),
                         r=[('ti', si), ('xc', hp, 0), ('xc', hp, C)], w=[('ti', si)])
                    S.op('vector', lambda e: e.tensor_tensor(out=tib[:, do:do + NB], in0=tib[:, do:do + NB], in1=trb[:, do:do + NB], op=ALU.mult),
                         r=[('ti', si), ('tr', si)], w=[('ti', si)])

                def stD2(h, d, si, s0):
                    do = si * NB
                    if d == 0:
                        S.op('vector', lambda e: e.tensor_tensor_scan(out=trb[:, do:do + NB], data0=ab[:, do:do + NB], data1=tib[:, do:do + NB], initial=0.0, op0=ALU.mult, op1=ALU.add),
                             r=[('a', si), ('ti', si)], w=[('tr', si)])
                    else:
                        S.op('vector', lambda e: e.tensor_tensor_scan(out=trb[:, do:do + C][:, ::-1], data0=ab[:, do:do + C][:, ::-1], data1=tib[:, do:do + C][:, ::-1],
                                                                      initial=0.0, op0=ALU.mult, op1=ALU.add), r=[('a', si), ('ti', si)], w=[('tr', si)])
                        S.op('vector', lambda e: e.tensor_tensor_scan(out=trb[:, do + C:do + NB][:, ::-1], data0=ab[:, do + C:do + NB][:, ::-1], data1=tib[:, do + C:do + NB][:, ::-1],
                                                                      initial=trb[:, do:do + 1], op0=ALU.mult, op1=ALU.add), r=[('a', si), ('ti', si), ('tr', si)], w=[('tr', si)])
                        d0 = s0 * NB
                        S.op('vector', lambda e: e.tensor_tensor(out=ab[:, do + C:do + NB], in0=trb[:, d0 + C:d0 + NB], in1=trb[:, do + C:do + NB], op=ALU.add),
                             r=[('tr', s0), ('tr', si), ('a', si)], w=[('a', si)])
                        S.op('vector', lambda e: e.tensor_tensor(out=Ylru[:, h * T:(h + 1) * T], in0=ab[:, do + C:do + NB], in1=Ylru[:, h * T:(h + 1) * T], op=ALU.mult),
                             r=[('a', si)], w=[('Yf', h)])

                units = [(h, d) for h in range(8) for d in range(2)]
                for s_ in range(17):
                    if s_ >= 1:
                        ph, pd = units[s_ - 1]
                        if pd == 1 and ph + 2 < 8:
                            stPa(ph + 2)
                    if s_ < 16:
                        stG(units[s_][0], units[s_][1], s_ % 3)
                    if s_ == 0:
                        stPa(1)
                        stPc(1, 0)
                        stPc(1, 1)
                    if s_ >= 1:
                        stD1(ph, pd, (s_ - 1) % 3)
                        stD2(ph, pd, (s_ - 1) % 3, (s_ - 2) % 3)
                        if pd == 1 and ph + 2 < 8:
                            stPc(ph + 2, 0)
                        if pd == 0 and 1 <= ph and ph + 1 < 8:
                            stPc(ph + 1, 1)
                    if s_ == 15:
                        S.op('gpsimd', lambda e: e.dma_start(out=k3(late['Wv'][:], 1024), in_=winv[:, :, 3072:4096]),
                             w=[('Wg', 0), ('Wg', 1), ('Wxs', 0), ('Wxs', 1), 'xa_c', 'xa_x'], dma=('wc', 'Wv'))
                        S.op('gpsimd', lambda e: e.dma_start(out=k3(late['Wu'][:], 1024), in_=winv[:, :, 2048:3072]),
                             w=['xa_x', ('xc', 0, 0), ('xc', 0, C), ('xcb', 0)], dma=('wc', 'Wu'))
            S.barrier()

        with contextlib.ExitStack() as _es:
            Wv = _es.enter_context(nc.sbuf_tensor("Wv", [128, 8 * 1024], BF16))
            late['Wv'] = Wv
            Wu = _es.enter_context(nc.sbuf_tensor("Wu", [128, 8 * 1024], BF16))
            late['Wu'] = Wu
            Wgb = _es.enter_context(nc.sbuf_tensor("Wgb", [128, 8 * 1024], BF16))
            wob = _es.enter_context(nc.sbuf_tensor("wob", [128, 16 * 1024], BF16))
            fgb = _es.enter_context(nc.sbuf_tensor("fgb", [128, D], F32))
            Bt = _es.enter_context(nc.sbuf_tensor("Bt", [128, 8 * 128], F32))
            wsTb = _es.enter_context(nc.sbuf_tensor("wsTb", [128, 1024], BF16))
            gv = _es.enter_context(nc.sbuf_tensor("gv", [128, D], F32))
            vhat = _es.enter_context(nc.sbuf_tensor("vhat", [128, 2 * D], BF16))
            gu = _es.enter_context(nc.sbuf_tensor("gu", [128, 8 * 256], F32))
            sgb = _es.enter_context(nc.sbuf_tensor("sgb", [128, 8 * 256], F32))
            Ysgu = _es.enter_context(nc.sbuf_tensor("Ysgu", [128, 8 * 256], BF16))
            xres = _es.enter_context(nc.sbuf_tensor("xres", [128, D], F32))
            xn = _es.enter_context(nc.sbuf_tensor("xn", [128, 2 * D], F32))
            st3 = _es.enter_context(nc.sbuf_tensor("st3", [128, 96], F32))
            S.op('gpsimd', lambda e: e.dma_start(out=wsTb[:], in_=wsT_d[:, :]), w=['wsTb'], dma=('wc', 'wsTb'))
            S.op('gpsimd', lambda e: e.dma_start(out=k3(Wgb[:], 1024), in_=winv[:, :, 4096:5120]), w=['Wgb'], dma=('wc', 'Wgb'))
            for hh in range(2):
                S.op('gpsimd', lambda e, hh=hh: e.dma_start(out=k3(wob[:, hh * 8192:(hh + 1) * 8192], 1024), in_=woutv[:, hh * 8:(hh + 1) * 8, :]),
                     w=[('wob', hh)], dma=('wc', 'wob', hh))
            ld(fgb[:], fgrep_d[:, :], ['fgb'])
            ld(xn[:, 0:D], sbrep_d[:, :], [('xn', 0)])

            def st_bt():
                def rs_mm(e):
                    i = None
                    for hf in range(2):
                        i = e.matmul(bank(6 + hf), lhsT=ones_b[:], rhs=wsTb[:, hf * 512:(hf + 1) * 512], start=True, stop=True)
                    return i
                S.op('tensor', rs_mm, r=['ones_b', 'wsTb'], w=[('ps', 6), ('ps', 7)])
                for g in range(8):
                    S.op('vector', lambda e, g=g: e.scalar_tensor_tensor(
                        out=Bt[:, g * 128:(g + 1) * 128], in0=ps[:, 6 * 512 + g * 128:6 * 512 + (g + 1) * 128],
                        scalar=par[:, P_LNB + g:P_LNB + g + 1], in1=xn[:, g * 128:(g + 1) * 128], op0=ALU.mult, op1=ALU.add),
                        r=[('ps', 6), ('ps', 7), ('xn', 0)], w=[('Bt', g)])

            ci = [0]

            def st_v(j, n):
                X0 = C + j * 256
                if True:
                    ci[0] += 1
                    so = (ci[0] % 4) * 16

                    def v_mm(e, n=n):
                        i = None
                        for hf in range(2):
                            for k in range(8):
                                i = e.matmul(bank(hf), lhsT=xmT[:, k * TT + X0 + n * 128:k * TT + X0 + (n + 1) * 128],
                                             rhs=Wv[:, k * 1024 + hf * 512:k * 1024 + (hf + 1) * 512], start=(k == 0), stop=(k == 7))
                        return i
                    S.op('tensor', v_mm, r=['Wv'], w=[('ps', 0), ('ps', 1)])
                    S.op('scalar', lambda e: e.activation(out=gv[:], in_=ps[:, 0:1024], func=AF.Gelu_apprx_tanh), r=[('ps', 0), ('ps', 1)], w=['gv'])
                    for hf in range(2):
                        S.op('vector', lambda e, hf=hf, so=so: e.bn_stats(out=st3[:, so + hf * 6:so + hf * 6 + 6], in_=gv[:, hf * 512:(hf + 1) * 512]),
                             r=['gv'], w=[('bst', so, hf)])
                    S.op('vector', lambda e, so=so: e.bn_aggr(out=st3[:, so + 12:so + 14], in_=st3[:, so:so + 12]), r=[('bst', so, 0), ('bst', so, 1)], w=[('mv', so)])
                    S.op('vector', lambda e, so=so: e.tensor_scalar(out=st3[:, so + 14:so + 15], in0=st3[:, so + 13:so + 14], scalar1=1e-5, scalar2=None, op0=ALU.add),
                         r=[('mv', so)], w=[('ve', so)])
                    S.op('gpsimd', lambda e, so=so: e.tensor_tensor(out=st3[:, so + 15:so + 16], in0=st3[:, so + 14:so + 15], in1=der[:, Q_NEGH:Q_NEGH + 1], op=ALU.pow),
                         r=[('ve', so)], w=[('rs', so)])
                    S.op('vector', lambda e, so=so, n=n: e.tensor_scalar(out=vhat[:, n * D:(n + 1) * D], in0=gv[:], scalar1=st3[:, so + 12:so + 13],
                                                                          scalar2=st3[:, so + 15:so + 16], op0=ALU.subtract, op1=ALU.mult),
                         r=['gv', ('mv', so), ('rs', so)], w=[('vhat', n)])

            def st_ug(j, which, pairs):
                X0 = C + j * 256
                for (Wt, wn, dst, dn, fn_) in (((Wu, 'Wu', gu, 'gu', AF.Gelu_apprx_tanh), (Wgb, 'Wgb', sgb, 'sgb', AF.Silu))[which],):
                    for gpair in pairs:
                        bk = 4 + gpair % 2

                        def ug_mm(e, gpair=gpair, bk=bk, Wt=Wt):
                            i = None
                            for g in (2 * gpair, 2 * gpair + 1):
                                co = (g % 2) * 256
                                for k in range(8):
                                    i = e.matmul(bank(bk, co, co + 256), lhsT=Wt[:, k * 1024 + g * 128:k * 1024 + (g + 1) * 128], rhs=xmT[:, k * TT + X0:k * TT + X0 + 256],
                                                 start=(k == 0), stop=(k == 7), skip_group_check=True)
                            return i
                        S.op('tensor', ug_mm, r=[wn], w=[('ps', bk)])
                        S.op('scalar', lambda e, gpair=gpair, bk=bk, dst=dst, fn_=fn_: e.activation(out=dst[:, gpair * 512:(gpair + 1) * 512], in_=bank(bk), func=fn_),
                             r=[('ps', bk)], w=[(dn, gpair)])

            def st_gg(j):
                S.op('vector', lambda e: e.tensor_tensor(out=gu[:], in0=gu[:], in1=sgb[:], op=ALU.mult),
                     r=[('gu', g) for g in range(4)] + [('sgb', g) for g in range(4)], w=[('gu', g) for g in range(4)])

            def st_mix(j, pairs, final):
                for gpair in pairs:
                    bk = 6 + gpair % 2

                    def mix_mm(e, gpair=gpair, bk=bk):
                        i = None
                        for g in (2 * gpair, 2 * gpair + 1):
                            co = (g % 2) * 256
                            for n in range(2):
                                i = e.matmul(bank(bk, co + n * 128, co + (n + 1) * 128), lhsT=vhat[:, n * D + g * 128:n * D + (g + 1) * 128], rhs=wsTb[:, g * 128:(g + 1) * 128],
                                             start=True, stop=True, skip_group_check=True)
                        return i
                    S.op('tensor', mix_mm, r=[('vhat', 0), ('vhat', 1), 'wsTb'], w=[('ps', bk)])
                    for g in (2 * gpair, 2 * gpair + 1):
                        co = (g % 2) * 256
                        for n in range(2):
                            S.op('vector', lambda e, g=g, bk=bk, co=co, n=n: e.scalar_tensor_tensor(
                                out=sgb[:, g * 256 + n * 128:g * 256 + (n + 1) * 128], in0=bank(bk, co + n * 128, co + (n + 1) * 128),
                                scalar=par[:, P_LNG + g:P_LNG + g + 1], in1=Bt[:, g * 128:(g + 1) * 128], op0=ALU.mult, op1=ALU.add),
                                r=[('ps', bk), 'p_lng', ('Bt', g)], w=[('sgb', gpair)])
                if final:
                    for hf_ in range(2):
                        c0, c1 = hf_ * 1024, (hf_ + 1) * 1024
                        S.op('vector', lambda e, c0=c0, c1=c1: e.tensor_tensor(out=Ysgu[:, c0:c1], in0=sgb[:, c0:c1], in1=gu[:, c0:c1], op=ALU.mult),
                             r=[('gu', 2 * hf_), ('gu', 2 * hf_ + 1), ('sgb', 2 * hf_), ('sgb', 2 * hf_ + 1)], w=[('Ysgu', hf_)])

            def st_o1(j, n):
                tg = j * 2 + n
                ob = tg % 2
                so = 64 + (tg % 4) * 4
                pb = 2 if n == 0 else 0
                S.op('sync', lambda e: e.dma_start(out=xres[:], in_=x_d[tg * 128:(tg + 1) * 128, :]), w=['xres'], dma=('xr', 0))

                def o_mm(e):
                    i = None
                    for hf in range(2):
                        for kc in range(16):
                            if kc < 8:
                                lt = Ylru[:, kc * T + j * 256 + n * 128:kc * T + j * 256 + (n + 1) * 128]
                            else:
                                lt = Ysgu[:, (kc - 8) * 256 + n * 128:(kc - 8) * 256 + (n + 1) * 128]
                            i = e.matmul(bank(pb + hf), lhsT=lt, rhs=wob[:, kc * 1024 + hf * 512:kc * 1024 + (hf + 1) * 512], start=(kc == 0), stop=(kc == 15))
                    return i
                S.op('tensor', o_mm, r=[('Ysgu', 0), ('Ysgu', 1), ('wob', 0), ('wob', 1)], w=[('ps', pb), ('ps', pb + 1)])
                S.op('vector', lambda e: e.tensor_tensor(out=xn[:, ob * D:(ob + 1) * D], in0=ps[:, pb * 512:(pb + 2) * 512], in1=gxb[:], op=ALU.mult),
                     r=[('ps', pb), ('ps', pb + 1), 'gxb'] + [('Bt', g) for g in range(8)], w=[('xn', ob)])
                S.op('vector', lambda e: e.tensor_tensor(out=xn[:, ob * D:(ob + 1) * D], in0=xn[:, ob * D:(ob + 1) * D], in1=xres[:], op=ALU.add),
                     r=[('xn', ob), 'xres'], w=[('xn', ob)])
                S.op('scalar', lambda e: e.activation(out=xres[:], in_=xn[:, ob * D:(ob + 1) * D], func=AF.Square, accum_out=st3[:, so:so + 1]),
                     r=[('xn', ob)], w=['xres', ('q0', so)])

            def st_o2(j, n):
                tg = j * 2 + n
                so = 64 + (tg % 4) * 4
                S.op('vector', lambda e: e.tensor_scalar(out=st3[:, so + 1:so + 2], in0=st3[:, so:so + 1], scalar1=1.0 / D, scalar2=1e-6, op0=ALU.mult, op1=ALU.add),
                     r=[('q0', so)], w=[('q1', so)])
                S.op('gpsimd', lambda e: e.tensor_tensor(out=st3[:, so + 2:so + 3], in0=st3[:, so + 1:so + 2], in1=der[:, Q_NEGH:Q_NEGH + 1], op=ALU.pow),
                     r=[('q1', so)], w=[('q2', so)])

            def st_o3(j, n):
                tg = j * 2 + n
                ob = tg % 2
                so = 64 + (tg % 4) * 4
                S.op('vector', lambda e: e.scalar_tensor_tensor(out=xn[:, ob * D:(ob + 1) * D], in0=xn[:, ob * D:(ob + 1) * D], scalar=st3[:, so + 2:so + 3],
                                                                in1=fgb[:], op0=ALU.mult, op1=ALU.mult),
                     r=[('xn', ob), ('q2', so), 'fgb'], w=[('xn', ob)])
                S.op('sync', lambda e: e.dma_start(out=out_d[tg * 128:(tg + 1) * 128, :], in_=xn[:, ob * D:(ob + 1) * D]), r=[('xn', ob)], dma=('st', ob))

            for j in range(8):
                st_v(j, 0)
                st_ug(j, 0, (0, 1))
                st_ug(j, 0, (2, 3))
                st_v(j, 1)
                st_ug(j, 1, (0, 1, 2, 3))
                st_gg(j)
                if j == 0:
                    st_bt()
                if j >= 1:
                    st_o1(j - 1, 0)
                st_mix(j, (0, 1), False)
                if j >= 1:
                    st_o1(j - 1, 1)
                st_mix(j, (2, 3), True)
                if j >= 1:
                    st_o2(j - 1, 0)
                    st_o2(j - 1, 1)
                    st_o3(j - 1, 0)
                    st_o3(j - 1, 1)
            for n in range(2):
                st_o1(7, n)
            for n in range(2):
                st_o2(7, n)
            for n in range(2):
                st_o3(7, n)

        keys = list(S.cnt.keys())
        with contextlib.ExitStack() as es:
            sems = {}
            for i, k in enumerate(keys):
                sems[k] = es.enter_context(nc.semaphore("s%d" % i))
            block = es.enter_context(nc.Block())
            S.emit(nc, block, sems)
    return nc


_NC_CACHE = {}


def kernel(x, c, ctx, c_ctx, ada_w, ada_b, norm_g, w_in, conv_w, conv_b, lru_wa, lru_ba, lru_wx, lru_bx,
           lru_lambda, sgu_ln_g, sgu_ln_b, sgu_w, sgu_b, w_out, final_g):
    f = lambda a: np.ascontiguousarray(np.asarray(a, dtype=np.float32))
    x = f(x); c = f(c); ctx = f(ctx); c_ctx = f(c_ctx)
    ada_w0 = f(ada_w[0]); ada_b0 = f(ada_b[0]); ng = f(norm_g[0]); w_in0 = f(w_in[0]); w_out0 = f(w_out[0])

    def colT(v, nchunk):
        return f(np.asarray(v, dtype=np.float32).reshape(nchunk, 128).T)
    adabT = colT(ada_b0, 24)
    bg_rep = f(np.broadcast_to(ada_b0[2048:3072][None, :], (128, 1024)))
    fg_rep = f(np.broadcast_to(np.asarray(final_g, dtype=np.float32)[None, :], (128, 1024)))
    ngT = colT(ng, 8)
    cwT = f(np.asarray(conv_w[0], dtype=np.float32).reshape(4, 8, 128).transpose(2, 1, 0).reshape(128, 32))
    cbT = colT(conv_b[0], 8)
    wa = np.asarray(lru_wa[0], dtype=np.float32)
    wx = np.asarray(lru_wx[0], dtype=np.float32)
    wg = f(np.stack([wa, wx], 0).transpose(3, 0, 1, 2, 4).reshape(128, 4096))
    baT = f(np.asarray(lru_ba[0], dtype=np.float32).transpose(2, 0, 1).reshape(128, 16))
    bxT = f(np.asarray(lru_bx[0], dtype=np.float32).transpose(2, 0, 1).reshape(128, 16))
    lamT = f(np.asarray(lru_lambda[0], dtype=np.float32).reshape(2, 8, 128).transpose(2, 0, 1).reshape(128, 16))
    lngT = colT(sgu_ln_g[0], 8)
    lnbT = colT(sgu_ln_b[0], 8)
    wsT = f(np.asarray(sgu_w[0], dtype=np.float32).transpose(2, 0, 1).reshape(128, 1024))
    sb_rep = f(np.broadcast_to(np.asarray(sgu_b[0], dtype=np.float32).reshape(1, 1024), (128, 1024)))

    if 'nc' not in _NC_CACHE:
        _NC_CACHE['nc'] = build_nc()
    nc = _NC_CACHE['nc']
    in_maps = []
    for b in range(NCORES):
        cT = f(np.stack([c[b].reshape(8, 128).T, c_ctx.reshape(8, 128).T], axis=2).reshape(128, 16))
        in_maps.append({
            "x": x[b], "ctx": ctx[b], "cT": cT, "ada_w": ada_w0, "adabT": adabT, "bg_rep": bg_rep, "fg_rep": fg_rep,
            "ngT": ngT, "w_in": w_in0, "w_out": w_out0, "cwT": cwT, "cbT": cbT, "wg": wg, "baT": baT, "bxT": bxT,
            "lamT": lamT, "lngT": lngT, "lnbT": lnbT, "wsT": wsT, "sb_rep": sb_rep,
        })
    res = run_bass_kernel_spmd(nc, in_maps, core_ids=list(range(NCORES)))
    return np.stack([np.asarray(r["out"], dtype=np.float32) for r in res.results], axis=0)
```
